# Optimizing a Trainium2 kernel written in Bass

```python
import math
import jax, jax.numpy as jnp
from jax import lax
import numpy as np

D_MODEL = 1024
BATCH = 16
SEQ = 2048
DEPTH = 2
DEC_BATCH = 32
DEC_SEQ = 32
PAST_LEN = 1024

CHUNK = 64
N_EVEN = (DEPTH + 1) // 2
N_ODD = DEPTH // 2
D_MIX = D_MODEL
D_FF = 2816
EPS = 1e-6

A_HEADS = 4
A_KEY = 128
A_VAL = 128
A_WIDTH = A_HEADS * A_VAL
HGRN_BLOCK = 64
B_GROUP_DIM = 16
B_WIDTH = D_MIX - A_WIDTH
B_GROUPS = B_WIDTH // B_GROUP_DIM
B_STATE = 64
DT_MIN = 1e-3
DT_MAX = 1e-1
LAMBDA_RE_CEIL = -1e-4
C_CHUNK = 128
C_WIDTH = 512
C_GROUPS = 4
C_GROUP_DIM = C_WIDTH // C_GROUPS
D_HEADS = 8
Q_LORA = 256
KV_LORA = 128
NOPE_DIM = 64
ROPE_DIM = 32
V_DIM = 64
QK_DIM = NOPE_DIM + ROPE_DIM
D_WIDTH = D_HEADS * V_DIM
ROPE_BASE = 10000.0
Q_BLOCK = 128

A_QK = A_HEADS * A_KEY
AB_IN = 2 * A_QK + 2 * A_WIDTH + B_WIDTH
AB_SPLITS = [A_QK, 2 * A_QK, 2 * A_QK + A_WIDTH, 2 * A_QK + 2 * A_WIDTH]
CD_IN = 2 * C_WIDTH + Q_LORA + KV_LORA + ROPE_DIM
CD_SPLITS = [C_WIDTH, 2 * C_WIDTH, 2 * C_WIDTH + Q_LORA, 2 * C_WIDTH + Q_LORA + KV_LORA]

kernel_name = "hybrid_streaming_encoder_step"


def rmsnorm(x, g):
    xf = x.astype(jnp.float32)
    y = xf * lax.rsqrt(jnp.mean(xf * xf, axis=-1, keepdims=True) + EPS)
    return y.astype(x.dtype) * g


def swiglu(h, w_gate, w_up, w_down):
    return (jax.nn.silu(h @ w_gate) * (h @ w_up)) @ w_down


def _hgrn2_block(S, blk):
    q, k, v, logf = blk
    L = q.shape[2]
    b = jnp.cumsum(logf, axis=2)
    causal = jnp.tril(jnp.ones((L, L), dtype=bool))[:, :, None]
    diff = b[:, :, :, None, :] - b[:, :, None, :, :]
    decay = jnp.exp(jnp.where(causal, diff, -jnp.inf))
    att = jnp.einsum('bhtsn,bhsn->bhts', q[:, :, :, None, :] * decay, k)
    o = jnp.einsum('bhts,bhsv->bhtv', att, v) + jnp.einsum('bhtn,bhnv->bhtv', q * jnp.exp(b), S)
    b_end = b[:, :, -1:, :]
    S_new = jnp.exp(b_end[:, :, 0, :, None]) * S + jnp.einsum('bhsn,bhsv->bhnv', k * jnp.exp(b_end - b), v)
    return S_new, o


def hgrn2_mix(q, fz, iv, gz, S0, lb, out_g):
    Bsz, T, _ = q.shape
    L = min(HGRN_BLOCK, T)
    nc = T // L
    f = lb + (1.0 - lb) * jax.nn.sigmoid(fz.astype(jnp.float32))

    def to_blocks(t, d):
        return t.astype(jnp.float32).reshape(Bsz, nc, L, A_HEADS, d).transpose(1, 0, 3, 2, 4)

    S_T, o = lax.scan(_hgrn2_block, S0.astype(jnp.float32),
                      (to_blocks(q, A_KEY), to_blocks(1.0 - f, A_KEY),
                       to_blocks(iv, A_VAL), to_blocks(jnp.log(f), A_KEY)))
    o = o.transpose(1, 0, 3, 2, 4).reshape(Bsz, T, A_HEADS, A_VAL)
    gate = jax.nn.sigmoid(gz.astype(jnp.float32)).reshape(Bsz, T, A_HEADS, A_VAL)
    y = rmsnorm(o, out_g) * gate
    return y.reshape(Bsz, T, A_WIDTH).astype(q.dtype), S_T


def _ssm_combine(e1, e2):
    a1, b1 = e1
    a2, b2 = e2
    return a2 * a1, a2 * b1 + b2


def s5_mix(u, x0_re, x0_im, lam_re, lam_im, log_dt, b_re, b_im, c_re, c_im, d, w_glu, b_glu):
    Bsz, T, _ = u.shape
    f32 = jnp.float32
    ug = u.astype(f32).reshape(Bsz, T, B_GROUPS, B_GROUP_DIM)
    lam = lax.complex(jnp.minimum(lam_re.astype(f32), LAMBDA_RE_CEIL), lam_im.astype(f32))
    dt = jnp.exp(log_dt.astype(f32))[:, None]
    lam_bar = jnp.exp(lam * dt)
    b_bar = ((lam_bar - 1.0) / lam)[:, :, None] * lax.complex(b_re.astype(f32), b_im.astype(f32))
    bu = jnp.einsum('gph,btgh->tbgp', b_bar, ug.astype(jnp.complex64))
    x0 = lax.complex(x0_re.astype(f32), x0_im.astype(f32))
    bu = bu.at[0].add(lam_bar * x0)
    a = jnp.broadcast_to(lam_bar, (T, 1, B_GROUPS, B_STATE))
    _, xs = lax.associative_scan(_ssm_combine, (a, bu), axis=0)
    y = (jnp.einsum('ghp,tbgp->btgh', c_re.astype(f32), xs.real)
         - jnp.einsum('ghp,tbgp->btgh', c_im.astype(f32), xs.imag)
         + d.astype(f32) * ug)
    z = jax.nn.gelu(y.reshape(Bsz, T, B_WIDTH))
    out = z * jax.nn.sigmoid(z @ w_glu.astype(f32) + b_glu.astype(f32))
    x_T = xs[-1]
    return out.astype(u.dtype), x_T.real, x_T.imag


def gmlp_mix(u, v, v_norm, w_s, b_s):
    Bsz, T, _ = v.shape
    L = min(T, C_CHUNK)
    vn = rmsnorm(v, v_norm)
    vc = vn.reshape(Bsz, T // L, L, C_GROUPS, C_GROUP_DIM)
    ws = jnp.tril(w_s[:, :L, :L])
    s = jnp.einsum('gts,bcsgd->bctgd', ws, vc) + b_s[:, :L].T[:, :, None]
    return u * s.reshape(Bsz, T, C_WIDTH), vn[:, T - L:]


def rope_cos_sin(pos):
    half = ROPE_DIM // 2
    inv = ROPE_BASE ** (-jnp.arange(half, dtype=jnp.float32) / half)
    ang = pos.astype(jnp.float32)[:, None] * inv[None, :]
    return jnp.cos(ang), jnp.sin(ang)


def apply_rope(x, cos, sin):
    half = ROPE_DIM // 2
    x1, x2 = x[..., :half], x[..., half:]
    cos = cos.astype(x.dtype)
    sin = sin.astype(x.dtype)
    return jnp.concatenate([x1 * cos - x2 * sin, x1 * sin + x2 * cos], axis=-1)


def attend(q, k, v, qpos, kpos):
    s = jnp.einsum('bqhd,bkhd->bhqk', q, k).astype(jnp.float32) * (QK_DIM ** -0.5)
    visible = (kpos[None, :] // CHUNK) <= (qpos[:, None] // CHUNK)
    p = jax.nn.softmax(jnp.where(visible, s, -jnp.inf), axis=-1).astype(v.dtype)
    return jnp.einsum('bhqk,bkhd->bqhd', p, v)


def attend_query_blocks(q, k, v, qpos, kpos):
    Bsz, T, H, Dq = q.shape
    nb = T // Q_BLOCK
    qb = q.reshape(Bsz, nb, Q_BLOCK, H, Dq).transpose(1, 0, 2, 3, 4)
    pb = qpos.reshape(nb, Q_BLOCK)
    o = lax.map(lambda blk: attend(blk[0], k, v, blk[1], kpos), (qb, pb))
    return o.transpose(1, 0, 2, 3, 4).reshape(Bsz, T, H, V_DIM)


def mla_mix(cq_lin, ckv_lin, kpe_lin, pos, past_ckv, past_kpe,
            q_norm, w_uq, kv_norm, w_ukv, q_gain, k_gain):
    Bsz, T, _ = cq_lin.shape
    cos, sin = rope_cos_sin(pos)
    q = (rmsnorm(cq_lin, q_norm) @ w_uq).reshape(Bsz, T, D_HEADS, QK_DIM)
    q = jnp.concatenate([q[..., :NOPE_DIM],
                         apply_rope(q[..., NOPE_DIM:], cos[:, None], sin[:, None])], axis=-1)
    q = rmsnorm(q, q_gain)
    ckv = rmsnorm(ckv_lin, kv_norm)
    kpe = apply_rope(kpe_lin, cos, sin)
    if past_ckv is None:
        ckv_all, kpe_all, kpos = ckv, kpe, pos
    else:
        n_past = past_ckv.shape[1]
        ckv_all = jnp.concatenate([past_ckv.astype(ckv.dtype), ckv], axis=1)
        kpe_all = jnp.concatenate([past_kpe.astype(kpe.dtype), kpe], axis=1)
        kpos = jnp.concatenate([jnp.arange(n_past, dtype=pos.dtype), pos])
    S = ckv_all.shape[1]
    kv = (ckv_all @ w_ukv).reshape(Bsz, S, D_HEADS, NOPE_DIM + V_DIM)
    k = jnp.concatenate([kv[..., :NOPE_DIM],
                         jnp.broadcast_to(kpe_all[:, :, None, :], (Bsz, S, D_HEADS, ROPE_DIM))], axis=-1)
    k = rmsnorm(k, k_gain)
    v = kv[..., NOPE_DIM:]
    if past_ckv is None:
        o = attend_query_blocks(q, k, v, pos, kpos)
    else:
        o = attend(q, k, v, pos, kpos)
    return o.reshape(Bsz, T, D_WIDTH), ckv, kpe


def trunk(x, pos, w, past):
    Bsz = x.shape[0]
    lb_all = jnp.cumsum(jax.nn.softmax(w['hgrn_lb_logits'].astype(jnp.float32), axis=0), axis=0)
    hgrn_new, s5re_new, s5im_new, gv_new, ckv_new, kpe_new = [], [], [], [], [], []
    for l in range(DEPTH):
        j = l // 2
        x = x + 0.5 * swiglu(rmsnorm(x, w['ffn1_norm'][l]), w['ffn1_w_gate'][l], w['ffn1_w_up'][l], w['ffn1_w_down'][l])
        h = rmsnorm(x, w['mix_norm'][l])
        if l % 2 == 0:
            q, fz, iv, gz, u = jnp.split(h @ w['ab_w_in'][j], AB_SPLITS, axis=-1)
            if past is None:
                S0 = jnp.zeros((Bsz, A_HEADS, A_KEY, A_VAL), jnp.float32)
                x0_re = jnp.zeros((Bsz, B_GROUPS, B_STATE), jnp.float32)
                x0_im = jnp.zeros((Bsz, B_GROUPS, B_STATE), jnp.float32)
            else:
                S0, x0_re, x0_im = past[0][j], past[1][j], past[2][j]
            a_out, S_T = hgrn2_mix(q, fz, iv, gz, S0, lb_all[l], w['hgrn_out_norm'][j])
            b_out, xr, xi = s5_mix(u, x0_re, x0_im, w['s5_lambda_re'][j], w['s5_lambda_im'][j], w['s5_log_dt'][j],
                                   w['s5_b_re'][j], w['s5_b_im'][j], w['s5_c_re'][j], w['s5_c_im'][j],
                                   w['s5_d'][j], w['s5_w_glu'][j], w['s5_b_glu'][j])
            mixed = jnp.concatenate([a_out, b_out], axis=-1) @ w['ab_w_out'][j]
            hgrn_new.append(S_T)
            s5re_new.append(xr)
            s5im_new.append(xi)
        else:
            uc, vc, cq, ckv_lin, kpe_lin = jnp.split(h @ w['cd_w_in'][j], CD_SPLITS, axis=-1)
            c_out, v_rows = gmlp_mix(jax.nn.gelu(uc), jax.nn.gelu(vc), w['gmlp_v_norm'][j],
                                     w['gmlp_w_s'][j], w['gmlp_b_s'][j])
            if past is None:
                past_ckv, past_kpe = None, None
            else:
                past_ckv, past_kpe = past[3][j], past[4][j]
            d_out, ckv, kpe = mla_mix(cq, ckv_lin, kpe_lin, pos, past_ckv, past_kpe,
                                      w['mla_q_norm'][j], w['mla_w_uq'][j], w['mla_kv_norm'][j],
                                      w['mla_w_ukv'][j], w['mla_q_gain'][j], w['mla_k_gain'][j])
            mixed = jnp.concatenate([c_out, d_out], axis=-1) @ w['cd_w_out'][j]
            gv_new.append(v_rows)
            ckv_new.append(ckv)
            kpe_new.append(kpe)
        x = x + mixed
        x = x + 0.5 * swiglu(rmsnorm(x, w['ffn2_norm'][l]), w['ffn2_w_gate'][l], w['ffn2_w_up'][l], w['ffn2_w_down'][l])
    return x, (jnp.stack(hgrn_new), jnp.stack(s5re_new), jnp.stack(s5im_new),
               jnp.stack(gv_new), jnp.stack(ckv_new), jnp.stack(kpe_new))


def setup_inputs(seed: int = 0) -> dict:
    key = jax.random.key(seed)
    keys = jax.random.split(key, 64)
    counter = [0]

    def nk():
        counter[0] += 1
        return keys[counter[0] - 1]

    def nrm(shape, scale=1.0):
        return scale * jax.random.normal(nk(), shape, jnp.float32)

    def gain(shape):
        return 1.0 + 0.01 * jax.random.normal(nk(), shape, jnp.float32)

    inp = {}
    inp['x_prompt'] = nrm((BATCH, SEQ, D_MODEL))
    inp['x_sample'] = nrm((DEC_BATCH, DEC_SEQ, D_MODEL))
    inp['state_hgrn'] = nrm((N_EVEN, DEC_BATCH, A_HEADS, A_KEY, A_VAL), 0.5)
    inp['state_s5_re'] = nrm((N_EVEN, DEC_BATCH, B_GROUPS, B_STATE), 0.1)
    inp['state_s5_im'] = nrm((N_EVEN, DEC_BATCH, B_GROUPS, B_STATE), 0.1)
    inp['cache_mla_ckv'] = nrm((N_ODD, DEC_BATCH, PAST_LEN, KV_LORA))
    inp['cache_mla_kpe'] = nrm((N_ODD, DEC_BATCH, PAST_LEN, ROPE_DIM))
    inp['ffn1_norm'] = gain((DEPTH, D_MODEL))
    inp['ffn1_w_gate'] = nrm((DEPTH, D_MODEL, D_FF), D_MODEL ** -0.5)
    inp['ffn1_w_up'] = nrm((DEPTH, D_MODEL, D_FF), D_MODEL ** -0.5)
    inp['ffn1_w_down'] = nrm((DEPTH, D_FF, D_MODEL), D_FF ** -0.5)
    inp['mix_norm'] = gain((DEPTH, D_MODEL))
    inp['ffn2_norm'] = gain((DEPTH, D_MODEL))
    inp['ffn2_w_gate'] = nrm((DEPTH, D_MODEL, D_FF), D_MODEL ** -0.5)
    inp['ffn2_w_up'] = nrm((DEPTH, D_MODEL, D_FF), D_MODEL ** -0.5)
    inp['ffn2_w_down'] = nrm((DEPTH, D_FF, D_MODEL), D_FF ** -0.5)
    inp['ab_w_in'] = nrm((N_EVEN, D_MODEL, AB_IN), D_MODEL ** -0.5)
    inp['hgrn_lb_logits'] = nrm((DEPTH + 1, A_QK), 0.5)
    inp['hgrn_out_norm'] = gain((N_EVEN, A_VAL))
    inp['s5_lambda_re'] = -0.5 + nrm((N_EVEN, B_GROUPS, B_STATE), 0.01)
    inp['s5_lambda_im'] = (jnp.broadcast_to(jnp.pi * jnp.arange(B_STATE, dtype=jnp.float32), (N_EVEN, B_GROUPS, B_STATE))
                           + nrm((N_EVEN, B_GROUPS, B_STATE), 0.01))
    inp['s5_log_dt'] = jax.random.uniform(nk(), (N_EVEN, B_GROUPS), jnp.float32,
                                          minval=math.log(DT_MIN), maxval=math.log(DT_MAX))
    inp['s5_b_re'] = nrm((N_EVEN, B_GROUPS, B_STATE, B_GROUP_DIM), (2 * B_GROUP_DIM) ** -0.5)
    inp['s5_b_im'] = nrm((N_EVEN, B_GROUPS, B_STATE, B_GROUP_DIM), (2 * B_GROUP_DIM) ** -0.5)
    inp['s5_c_re'] = nrm((N_EVEN, B_GROUPS, B_GROUP_DIM, B_STATE), B_STATE ** -0.5)
    inp['s5_c_im'] = nrm((N_EVEN, B_GROUPS, B_GROUP_DIM, B_STATE), B_STATE ** -0.5)
    inp['s5_d'] = nrm((N_EVEN, B_GROUPS, B_GROUP_DIM))
    inp['s5_w_glu'] = nrm((N_EVEN, B_WIDTH, B_WIDTH), B_WIDTH ** -0.5)
    inp['s5_b_glu'] = nrm((N_EVEN, B_WIDTH), 0.01)
    inp['ab_w_out'] = nrm((N_EVEN, D_MIX, D_MODEL), D_MIX ** -0.5)
    inp['cd_w_in'] = nrm((N_ODD, D_MODEL, CD_IN), D_MODEL ** -0.5)
    inp['gmlp_v_norm'] = gain((N_ODD, C_WIDTH))
    inp['gmlp_w_s'] = nrm((N_ODD, C_GROUPS, C_CHUNK, C_CHUNK), C_CHUNK ** -0.5)
    inp['gmlp_b_s'] = gain((N_ODD, C_GROUPS, C_CHUNK))
    inp['mla_q_norm'] = gain((N_ODD, Q_LORA))
    inp['mla_w_uq'] = nrm((N_ODD, Q_LORA, D_HEADS * QK_DIM), Q_LORA ** -0.5)
    inp['mla_kv_norm'] = gain((N_ODD, KV_LORA))
    inp['mla_w_ukv'] = nrm((N_ODD, KV_LORA, D_HEADS * (NOPE_DIM + V_DIM)), KV_LORA ** -0.5)
    inp['mla_q_gain'] = gain((N_ODD, QK_DIM))
    inp['mla_k_gain'] = gain((N_ODD, QK_DIM))
    inp['cd_w_out'] = nrm((N_ODD, D_MIX, D_MODEL), D_MIX ** -0.5)
    return inp


def reference(x_prompt, x_sample, state_hgrn, state_s5_re, state_s5_im, cache_mla_ckv, cache_mla_kpe,
              ffn1_norm, ffn1_w_gate, ffn1_w_up, ffn1_w_down, mix_norm,
              ffn2_norm, ffn2_w_gate, ffn2_w_up, ffn2_w_down,
              ab_w_in, hgrn_lb_logits, hgrn_out_norm, s5_lambda_re, s5_lambda_im, s5_log_dt,
              s5_b_re, s5_b_im, s5_c_re, s5_c_im, s5_d, s5_w_glu, s5_b_glu, ab_w_out,
              cd_w_in, gmlp_v_norm, gmlp_w_s, gmlp_b_s, mla_q_norm, mla_w_uq, mla_kv_norm,
              mla_w_ukv, mla_q_gain, mla_k_gain, cd_w_out):
    w = dict(ffn1_norm=ffn1_norm, ffn1_w_gate=ffn1_w_gate, ffn1_w_up=ffn1_w_up, ffn1_w_down=ffn1_w_down,
             mix_norm=mix_norm, ffn2_norm=ffn2_norm, ffn2_w_gate=ffn2_w_gate, ffn2_w_up=ffn2_w_up,
             ffn2_w_down=ffn2_w_down, ab_w_in=ab_w_in, hgrn_lb_logits=hgrn_lb_logits,
             hgrn_out_norm=hgrn_out_norm, s5_lambda_re=s5_lambda_re, s5_lambda_im=s5_lambda_im,
             s5_log_dt=s5_log_dt, s5_b_re=s5_b_re, s5_b_im=s5_b_im, s5_c_re=s5_c_re, s5_c_im=s5_c_im,
             s5_d=s5_d, s5_w_glu=s5_w_glu, s5_b_glu=s5_b_glu, ab_w_out=ab_w_out, cd_w_in=cd_w_in,
             gmlp_v_norm=gmlp_v_norm, gmlp_w_s=gmlp_w_s, gmlp_b_s=gmlp_b_s, mla_q_norm=mla_q_norm,
             mla_w_uq=mla_w_uq, mla_kv_norm=mla_kv_norm, mla_w_ukv=mla_w_ukv, mla_q_gain=mla_q_gain,
             mla_k_gain=mla_k_gain, cd_w_out=cd_w_out)
    past_len = cache_mla_ckv.shape[2]
    pos_prompt = jnp.arange(x_prompt.shape[1], dtype=jnp.int32)
    pos_sample = past_len + jnp.arange(x_sample.shape[1], dtype=jnp.int32)
    y_prompt, st_p = trunk(x_prompt, pos_prompt, w, None)
    y_sample, st_s = trunk(x_sample, pos_sample, w,
                           (state_hgrn, state_s5_re, state_s5_im, cache_mla_ckv, cache_mla_kpe))
    hgrn_p, s5re_p, s5im_p, gv_p, ckv_p, kpe_p = st_p
    hgrn_s, s5re_s, s5im_s, gv_s, ckv_s, kpe_s = st_s
    return (y_prompt, y_sample, hgrn_p, hgrn_s, s5re_p, s5im_p, s5re_s, s5im_s,
            gv_p, gv_s, ckv_p, kpe_p, ckv_s, kpe_s)
```

```python
from contextlib import ExitStack
import numpy as np
import concourse.bass as bass
import concourse.mybir as mybir
from concourse.bass_utils import run_bass_kernel_spmd

F32 = mybir.dt.float32
BF16 = mybir.dt.bfloat16
AF = mybir.ActivationFunctionType
ALU = mybir.AluOpType
AX = mybir.AxisListType

N_CORES = 8
D = 1024
DFF = 2816
NJ = DFF // 128
SEQ = 2048
EPS = 1e-6
PAST = 1024

EPOCH = 30000
NDMA = 8


class Tok:
    __slots__ = ("w", "r")

    def __init__(self):
        self.w = None
        self.r = {}


def toks(n):
    return [Tok() for _ in range(n)]


class Sched:
    ENGS = ["pe", "act", "dve", "pool", "sp"]

    def __init__(self, nc):
        self.nc = nc
        self.q = {e: [] for e in self.ENGS}
        self.cnt = {}
        self.seen = {e: {} for e in self.ENGS}
        self.icount = {e: 0 for e in self.ENGS}
        self.dma_rr = {e: 0 for e in self.ENGS}
        self.keys = []

    def _key_count(self, key):
        if key not in self.cnt:
            self.cnt[key] = 0
            self.keys.append(key)
        return self.cnt[key]

    def _need(self, eng, waits, key, val, same_ok):
        if val <= 0:
            return
        if key[0] == eng and (eng == "pe" or same_ok):
            return
        if self.seen[eng].get(key, 0) >= val:
            return
        if waits.get(key, 0) < val:
            waits[key] = val

    def op(self, eng, fn, reads=(), writes=(), dma=False):
        waits = {}
        for t in reads:
            if t.w is not None:
                self._need(eng, waits, t.w[0], t.w[1], False)
        for t in writes:
            if t.w is not None:
                self._need(eng, waits, t.w[0], t.w[1], True)
            for k, v in t.r.items():
                self._need(eng, waits, k, v, True)
        if dma:
            slot = self.dma_rr[eng] % NDMA
            self.dma_rr[eng] += 1
            key = ("dma_" + eng, slot)
            prev = self._key_count(key)
            if prev > 0 and self.seen[eng].get(key, 0) < prev:
                waits[key] = max(waits.get(key, 0), prev)
            inc = 16
        else:
            ep = self.icount[eng] // EPOCH
            self.icount[eng] += 1
            key = (eng, ep)
            self._key_count(key)
            inc = 1
        for k, v in waits.items():
            self.seen[eng][k] = v
        c = self.cnt[key] + inc
        self.cnt[key] = c
        for t in reads:
            if t.r.get(key, 0) < c:
                t.r[key] = c
        for t in writes:
            t.w = (key, c)
            t.r = {}
        self.q[eng].append((waits, fn, key, inc))

    def finish(self):
        for eng in self.ENGS:
            waits = {}
            for key in self.keys:
                if key[0] == "dma_" + eng and self.cnt[key] > 0:
                    waits[key] = self.cnt[key]
            if waits:
                self.q[eng].append((waits, None, None, 0))

    def build(self):
        nc = self.nc
        with ExitStack() as es:
            sems = {}
            for key in self.keys:
                sems[key] = es.enter_context(nc.semaphore("s_%s_%s" % (key[0], key[1])))
            block = es.enter_context(nc.Block())

            def replay(eng_name):
                def run(e):
                    for waits, fn, key, inc in self.q[eng_name]:
                        for k, v in waits.items():
                            e.wait_ge(sems[k], v)
                        if fn is not None:
                            fn(e).then_inc(sems[key], inc)
                return run

            if self.q["pe"]:
                block.tensor(replay("pe"))
            if self.q["act"]:
                block.scalar(replay("act"))
            if self.q["dve"]:
                block.vector(replay("dve"))
            if self.q["pool"]:
                block.gpsimd(replay("pool"))
            if self.q["sp"]:
                block.sync(replay("sp"))


def alias_barrier(dst, src):
    merged = {}
    for t in src:
        if t.w is not None:
            k, v = t.w
            if merged.get(k, 0) < v:
                merged[k] = v
        for k, v in t.r.items():
            if merged.get(k, 0) < v:
                merged[k] = v
    for t in dst:
        for k, v in merged.items():
            if t.r.get(k, 0) < v:
                t.r[k] = v


def fm_unit(w, col0, nchunks=4):
    k = w.shape[0] // 128
    blk = w[:, col0:col0 + 128 * nchunks].reshape(k, 128, nchunks, 128)
    return np.ascontiguousarray(blk.transpose(1, 2, 0, 3))


def tm_unit(w, col0, ncols):
    k = w.shape[0] // 128
    out = np.zeros((128, k, 512), np.float32)
    out[:, :, :ncols] = w[:, col0:col0 + ncols].reshape(k, 128, ncols).transpose(1, 0, 2)
    return out


def build_program(stop="full", debug=False):
    nc = bass.Bass("TRN2", target_bir_lowering=False)
    S = Sched(nc)
    es = ExitStack()

    def dram_in(name, shape):
        return nc.dram_tensor(name, list(shape), F32, kind="ExternalInput").ap()

    def dram_out(name, shape):
        return nc.dram_tensor(name, list(shape), F32, kind="ExternalOutput").ap()

    def sb(name, shape, dt=F32):
        return es.enter_context(nc.sbuf_tensor(name, list(shape), dt))

    xp = dram_in("xp", [2, SEQ, D])
    xs = dram_in("xs", [128, D])
    wstream = dram_in("wstream", [NUNITS, 128, 4096])
    consts = dram_in("consts", [128, NCONST])
    params = dram_in("params", [128, NPARAM])
    wsmall_d = dram_in("wsmall", [128, NSMALL])
    cmats_d = dram_in("cmats", [128, 4096])
    gws_d = dram_in("gws", [128, 512])
    s5x0_d = dram_in("s5x0", [128, 128])
    hs0_d = dram_in("hs0", [4, 4, 128, 128])
    pckv_d = dram_in("pckv", [4, PAST, 128])
    pkpe_d = dram_in("pkpe", [4, PAST, 32])

    yp = dram_out("yp", [2, SEQ, D])
    ys = dram_out("ys", [128, D])
    o_hgp = dram_out("o_hgp", [2, 4, 128, 128])
    o_hgs = dram_out("o_hgs", [4, 4, 128, 128])
    o_s5p = dram_out("o_s5p", [2, 2, 16, 128])
    o_s5s = dram_out("o_s5s", [4, 2, 16, 128])
    o_gvp = dram_out("o_gvp", [2, 128, 512])
    o_gvs = dram_out("o_gvs", [128, 512])
    o_ckvp = dram_out("o_ckvp", [2, SEQ, 128])
    o_kpep = dram_out("o_kpep", [2, SEQ, 32])
    o_ckvs = dram_out("o_ckvs", [128, 128])
    o_kpes = dram_out("o_kpes", [128, 32])

    x = sb("x", [128, 4, D]); x_t = toks(4)
    hT = sb("hT", [128, 8, 512], BF16); hT_t = toks(4)
    ring = [sb("ring%d" % i, [128, 4096], BF16) for i in range(NRING)]
    ring_t = toks(NRING)
    ring_rr = [0]
    ARENA = 21760
    arena = sb("arena", [128, ARENA], BF16)
    arena_off = [0]
    scratch_toks = []

    def carve(shape, dt=F32, ntok=1):
        n = 1
        for s_ in shape[1:]:
            n *= s_
        nb16 = n * (2 if dt == F32 else 1)
        nb16 = (nb16 + 1) // 2 * 2
        off = arena_off[0]
        arena_off[0] += nb16
        assert arena_off[0] <= ARENA, ("arena overflow", arena_off[0])
        v = arena[:, off:off + nb16]
        if dt == F32:
            v = v.bitcast(F32)
        v = v[:, 0:n]
        if len(shape) == 3:
            v = v.rearrange("p (a b) -> p a b", a=shape[1])
        elif len(shape) == 4:
            v = v.rearrange("p (a b c) -> p a b c", a=shape[1], b=shape[2])
        tk = toks(ntok)
        scratch_toks.extend(tk)
        return (v, tk[0]) if ntok == 1 else (v, tk)

    actb = arena[:, 0:NJ * 512].rearrange("p (j n) -> p j n", j=NJ); act_t = toks(NJ)
    sg = arena[:, NJ * 512:NJ * 512 + 2048].bitcast(F32).rearrange("p (s n) -> p s n", s=2); sg_t = toks(2)
    ffn_toks = act_t + sg_t
    cst = sb("cst", [128, NCONST]); cst_t = Tok()
    prm = sb("prm", [128, NPARAM]); prm_t = Tok()
    wsm = sb("wsm", [128, NSMALL], BF16); wsm_t = Tok()
    ident = sb("ident", [128, 128], BF16); ident_t = Tok()
    xn = sb("xn", [128, 2, D], BF16); xn_t = toks(2)
    stat = sb("stat", [128, 8, 4]); stat_t = toks(8)
    stat_rr = [0]

    ps = [es.enter_context(nc.psum_tensor("ps%d" % b, [128, 512], F32)) for b in range(8)]
    ps_t = toks(8)
    ps_rr = [0]

    ps_lo = [0]

    ps5 = [0]
    psu = [0]

    def pbank5():
        b = 5 + ps5[0] % 2
        ps5[0] += 1
        return b

    def pbu():
        b = psu[0] % 4
        psu[0] += 1
        return b

    def pbank(lo=False):
        if lo:
            b = ps_lo[0] % 4
            ps_lo[0] += 1
            return b
        b = ps_rr[0] % 8
        ps_rr[0] += 1
        return b

    def dma(eng, out, in_, reads=(), writes=()):
        S.op(eng, lambda e: e.dma_start(out=out, in_=in_), reads=reads, writes=writes, dma=True)

    def mm(out, lhsT, rhs, start, stop, reads, writes, tp=None):
        if tp is None:
            S.op("pe", lambda e: e.matmul(out, lhsT=lhsT, rhs=rhs, start=start, stop=stop), reads=reads, writes=writes)
        else:
            S.op("pe", lambda e: e.matmul(out, lhsT=lhsT, rhs=rhs, start=start, stop=stop, tile_position=tp),
                 reads=reads, writes=writes)

    def tr(out, in_, reads, writes, idn=None):
        idap = ident[:] if idn is None else idn
        S.op("pe", lambda e: e.transpose(out, in_, idap), reads=list(reads) + [ident_t, cst_t], writes=writes)

    def act(out, in_, func, reads, writes, scale=1.0, bias=0.0, accum=None):
        if accum is None:
            S.op("act", lambda e: e.activation(out=out, in_=in_, func=func, bias=bias, scale=scale),
                 reads=reads, writes=writes)
        else:
            S.op("act", lambda e: e.activation(out=out, in_=in_, func=func, bias=bias, scale=scale, accum_out=accum),
                 reads=reads, writes=writes)

    def tt(out, in0, in1, op, reads, writes, eng="dve"):
        S.op(eng, lambda e: e.tensor_tensor(out=out, in0=in0, in1=in1, op=op), reads=reads, writes=writes)

    def ts(out, in0, s1, s2, op0, op1, reads, writes, eng="dve"):
        if s2 is None:
            S.op(eng, lambda e: e.tensor_scalar(out=out, in0=in0, scalar1=s1, scalar2=None, op0=op0),
                 reads=reads, writes=writes)
        else:
            S.op(eng, lambda e: e.tensor_scalar(out=out, in0=in0, scalar1=s1, scalar2=s2, op0=op0, op1=op1),
                 reads=reads, writes=writes)

    def stt(out, in0, scalar, in1, op0, op1, reads, writes):
        S.op("dve", lambda e: e.scalar_tensor_tensor(out=out, in0=in0, scalar=scalar, in1=in1, op0=op0, op1=op1),
             reads=reads, writes=writes)

    def cp(out, in_, reads, writes, eng="dve"):
        S.op(eng, lambda e: e.tensor_copy(out=out, in_=in_), reads=reads, writes=writes)

    def recip(out, in_, reads, writes):
        S.op("dve", lambda e: e.reciprocal(out=out, in_=in_), reads=reads, writes=writes)

    def scan(out, d0, d1, init, reads, writes):
        S.op("dve", lambda e: e.tensor_tensor_scan(out=out, data0=d0, data1=d1, initial=init, op0=ALU.mult, op1=ALU.add),
             reads=reads, writes=writes)

    def memset(ap, val, writes, eng="dve"):
        S.op(eng, lambda e: e.memset(ap, val), writes=writes)

    def reduce_sum(out, in_, reads, writes):
        S.op("dve", lambda e: e.tensor_reduce(out=out, in_=in_, axis=AX.X, op=ALU.add), reads=reads, writes=writes)

    def load_unit(uidx):
        sl = ring_rr[0] % NRING
        ring_rr[0] += 1
        dma("pool", ring[sl][:], wstream[uidx], writes=[ring_t[sl]])
        return ring[sl], ring_t[sl]

    def new_stat():
        i = stat_rr[0] % 8
        stat_rr[0] += 1
        return stat[:, i, :], stat_t[i]

    class _Stop(Exception):
        pass

    def ck(n):
        if stop == "ck%d" % n:
            raise _Stop()

    def bc(ap2, shape, axis):
        return ap2.unsqueeze(axis).to_broadcast(shape)

    dma("sp", cst[:], consts, writes=[cst_t])
    dma("sp", prm[:], params, writes=[prm_t])
    dma("pool", wsm[:], wsmall_d, writes=[wsm_t])
    cp(ident[:], cst[:, C_IDENT:C_IDENT + 128], [cst_t], [ident_t])
    ident32 = cst[:, C_IDENT:C_IDENT + 128]

    def rmsnorm_to_hT(NT, gcol):
        for a in range(NT):
            st, st_t = new_stat()
            b = a % 2
            act(xn[:, b, :], x[:, a, :], AF.Square, [x_t[a]], [xn_t[b], st_t], accum=st[:, 0:1])
            act(st[:, 1:2], st[:, 0:1], AF.Sqrt, [st_t], [st_t], scale=1.0 / D, bias=EPS)
            recip(st[:, 2:3], st[:, 1:2], [st_t], [st_t])
            act(xn[:, b, :], x[:, a, :], AF.Copy, [x_t[a], st_t], [xn_t[b]], scale=st[:, 2:3])
            pv = ps[b][:].bitcast(BF16)
            for c in range(8):
                tr(pv[:, c * 128:(c + 1) * 128], xn[:, b, c * 128:(c + 1) * 128], [xn_t[b]], [ps_t[b]])
            tt(hT[:, :, a * 128:(a + 1) * 128], pv.rearrange("p (c t) -> p c t", c=8),
               bc(prm[:, gcol:gcol + 8], [128, 8, 128], 2), ALU.mult, [ps_t[b], prm_t], [hT_t[a]])

    def ffn(NT, ubase):
        T = NT * 128
        for u in range(NJ // 2):
            rb, rt = load_unit(ubase + u)
            rv = rb[:].rearrange("p (q k f) -> p q k f", q=4, k=8)
            for jj in range(2):
                j = 2 * u + jj
                bg, bu = (j % 2), 2 + (j % 2)
                for k in range(8):
                    mm(ps[bg][:, 0:T], rv[:, 2 * jj, k, :], hT[:, k, 0:T], k == 0, k == 7,
                       [rt] + hT_t[:NT], [ps_t[bg]])
                for k in range(8):
                    mm(ps[bu][:, 0:T], rv[:, 2 * jj + 1, k, :], hT[:, k, 0:T], k == 0, k == 7,
                       [rt] + hT_t[:NT], [ps_t[bu]])
                s = j % 2
                act(sg[:, s, 0:T], ps[bg][:, 0:T], AF.Silu, [ps_t[bg]], [sg_t[s]])
                tt(actb[:, j, 0:T], sg[:, s, 0:T], ps[bu][:, 0:T], ALU.mult, [sg_t[s], ps_t[bu]], [act_t[j]])
        for hf in range(2):
            bb = 4 if hf == 0 else 0
            for ud in range(3):
                rb, rt = load_unit(ubase + 11 + 3 * hf + ud)
                rv = rb[:].rearrange("p (j n) -> p j n", j=8)
                for jj in range(8 if ud < 2 else 6):
                    j = 8 * ud + jj
                    for a in range(NT):
                        mm(ps[bb + a][:, :], actb[:, j, a * 128:(a + 1) * 128], rv[:, jj, :],
                           j == 0, j == NJ - 1, [act_t[j], rt], [ps_t[bb + a]])
            for a in range(NT):
                stt(x[:, a, hf * 512:(hf + 1) * 512], ps[bb + a][:, :], 0.5, x[:, a, hf * 512:(hf + 1) * 512],
                    ALU.mult, ALU.add, [ps_t[bb + a], x_t[a]], [x_t[a]])

    def out_proj(NT, a0, ubase, aT, aT_t):
        for hf in range(2):
            rb, rt = load_unit(ubase + hf)
            rv = rb[:].rearrange("p (k n) -> p k n", k=8)
            for a in range(NT):
                b = pbank()
                for k in range(8):
                    mm(ps[b][:, :], aT[:, k, a * 128:(a + 1) * 128], rv[:, k, :], k == 0, k == 7,
                       [rt] + list(aT_t), [ps_t[b]])
                tt(x[:, a0 + a, hf * 512:(hf + 1) * 512], ps[b][:, :], x[:, a0 + a, hf * 512:(hf + 1) * 512], ALU.add,
                   [ps_t[b], x_t[a0 + a]], [x_t[a0 + a]])

    pvv = sb("pvv", [128, 256]); pvv_t = Tok()
    V_LB, V_OML, V_R, V_CA, V_CB, V_RN2, V_TH = 0, 4, 8, 24, 40, 56, 72
    V_TMP = 88
    cosT = sb("cosT", [128, 16, 128]); sinT = sb("sinT", [128, 16, 128]); tab_t = Tok()
    cre = sb("cre", [128, 16, 128], BF16); ncim = sb("ncim", [128, 16, 128], BF16); cm_t = Tok()
    ones128 = sb("ones128", [128, 128], BF16); ones_t = Tok()
    maskP = sb("maskP", [128, 128], BF16); maskS = sb("maskS", [128, 128], BF16); mask_t = Tok()
    Sst = sb("Sst", [128, 4, 128]); Sst_t = toks(4)
    Sbf = sb("Sbf", [128, 4, 128], BF16); Sbf_t = toks(4)
    car = sb("car", [128, 2, 16, 4]); car_t = Tok()
    x0p = sb("x0p", [128, 2, 16, 4]); x0p_t = Tok()
    MT = 256
    abT, abT_t = carve([128, 8, MT], BF16, 8)
    qs, qk_t = carve([128, 4, MT], BF16, 4)
    ks, _ks_t = carve([128, 4, MT], BF16)
    gateT, gate_t = carve([128, 4, MT], BF16, 4)
    ivb, iv_t = carve([128, 2, 512], BF16, 2)
    uT, uT_t = carve([128, 4, MT], F32, 4)
    ht, ht_t = [], []
    for _i in range(4):
        _v, _t = carve([128, MT]); ht.append(_v); ht_t.append(_t)
    lt, lt_t = [], []
    for _i in range(6):
        _v, _t = carve([128, 512]); lt.append(_v); lt_t.append(_t)
    lt_rr = [0]
    xrb, xb_t = carve([128, 512], BF16)
    xib, _xib_t = carve([128, 512], BF16)
    ub, ub_t = carve([128, 128], BF16)
    ub2, ub2_t = carve([128, 128], BF16)
    ubs, ubs_t = [ub, ub2], [ub_t, ub2_t]
    sm, sm_t = [], []
    for _i in range(6):
        _v, _t = carve([128, 128]); sm.append(_v); sm_t.append(_t)
    sm_rr = [0]
    smb, smb_t = [], []
    for _i in range(4):
        _v, _t = carve([128, 128], BF16); smb.append(_v); smb_t.append(_t)
    smb_rr = [0]
    khm, khm_t = carve([128, 4, 128], BF16)
    zf, z_t = carve([128, 4, 128], F32, 4)
    zb, _zb_t = carve([128, 4, 128], BF16)
    L0_toks = list(scratch_toks)
    L1_toks_ref = []
    L0_end = arena_off[0]

    def nsm():
        i = sm_rr[0] % 6
        sm_rr[0] += 1
        return sm[i], sm_t[i]

    def nsmb():
        i = smb_rr[0] % 4
        smb_rr[0] += 1
        return smb[i], smb_t[i]

    def nlt():
        i = lt_rr[0] % 6
        lt_rr[0] += 1
        return lt[i], lt_t[i]

    def prep_layer0():
        P = prm
        V = pvv
        rd = [prm_t, pvv_t, cst_t]
        wr = [pvv_t]

        def col(c0, n=16):
            return V[:, c0:c0 + n]
        T0 = [col(V_TMP + 16 * k) for k in range(10)]
        e3 = V[:, 248:256]
        act(V[:, 232:244], P[:, P_LB:P_LB + 12], AF.Exp, rd, wr)
        tt(T0[0][:, 0:4], V[:, 232:236], V[:, 236:240], ALU.add, rd, wr)
        tt(T0[0][:, 0:4], T0[0][:, 0:4], V[:, 240:244], ALU.add, rd, wr)
        recip(T0[0][:, 4:8], T0[0][:, 0:4], rd, wr)
        tt(col(V_LB, 4), V[:, 232:236], T0[0][:, 4:8], ALU.mult, rd, wr)
        ts(col(V_OML, 4), col(V_LB, 4), -1.0, 1.0, ALU.mult, ALU.add, rd, wr)
        dtc = T0[1]
        act(dtc, P[:, P_LOGDT:P_LOGDT + 16], AF.Exp, rd, wr)
        lre = T0[2]
        ts(lre, P[:, P_LRE:P_LRE + 16], -1e-4, None, ALU.min, None, rd, wr)
        tt(T0[3], lre, dtc, ALU.mult, rd, wr)
        act(col(V_R), T0[3], AF.Exp, rd, wr)
        tt(col(V_TH), P[:, P_LIM:P_LIM + 16], dtc, ALU.mult, rd, wr)
        for q4 in range(4):
            isl = slice(4 * q4, 4 * q4 + 4)
            A, At = lt[0], lt_t[0]
            Bt_, Btt = lt[1], lt_t[1]
            Ci, Cit = lt[2], lt_t[2]
            Av = A.rearrange("p (i j) -> p i j", i=4)
            tt(Av, bc(V[:, V_TH + 4 * q4:V_TH + 4 * q4 + 4], [128, 4, 128], 2),
               bc(cst[:, C_JV:C_JV + 128], [128, 4, 128], 1), ALU.mult, rd, [At])
            for which, tab in ((0, sinT), (1, cosT)):
                off = 64.0 + (0.25 if which == 1 else 0.0)
                ts(Bt_, A, 1.0 / (2.0 * np.pi), off, ALU.mult, ALU.add, [At], [Btt])
                S.op("dve", lambda e, o=Ci.bitcast(mybir.dt.int32), i_=Bt_: e.tensor_copy(out=o, in_=i_),
                     reads=[Btt], writes=[Cit])
                D2, D2t = lt[3], lt_t[3]
                S.op("dve", lambda e, o=D2, i_=Ci.bitcast(mybir.dt.int32): e.tensor_copy(out=o, in_=i_),
                     reads=[Cit], writes=[D2t])
                tt(Bt_, Bt_, D2, ALU.subtract, [Btt, D2t], [Btt])
                ts(D2, Bt_, 0.5, None, ALU.is_ge, None, [Btt], [D2t])
                tt(Bt_, Bt_, D2, ALU.subtract, [Btt, D2t], [Btt])
                act(tab[:, isl, :], Bt_.rearrange("p (i j) -> p i j", i=4), AF.Sin, [Btt], [tab_t],
                    scale=2.0 * np.pi)
        rd2 = rd + [tab_t]
        ur, ui_, tr_, ti_, pr, pi_, t1_, t2_ = T0[3], T0[0], T0[4], T0[5], T0[6], T0[7], T0[8], T0[9]
        ts(ur, T0[3], 1.0 / 256.0, None, ALU.mult, None, rd, wr)
        ts(ui_, col(V_TH), 1.0 / 256.0, None, ALU.mult, None, rd, wr)
        ts(tr_, ur, 1.0 / 6.0, 1.0, ALU.mult, ALU.add, rd, wr)
        ts(ti_, ui_, 1.0 / 6.0, None, ALU.mult, None, rd, wr)
        for kk in (5.0, 4.0, 3.0, 2.0, None):
            tt(t1_, ur, tr_, ALU.mult, rd, wr)
            tt(t2_, ui_, ti_, ALU.mult, rd, wr)
            tt(pr, t1_, t2_, ALU.subtract, rd, wr)
            tt(t1_, ur, ti_, ALU.mult, rd, wr)
            tt(t2_, ui_, tr_, ALU.mult, rd, wr)
            tt(pi_, t1_, t2_, ALU.add, rd, wr)
            if kk is not None:
                ts(tr_, pr, 1.0 / kk, 1.0, ALU.mult, ALU.add, rd, wr)
                ts(ti_, pi_, 1.0 / kk, None, ALU.mult, None, rd, wr)
        for _sq in range(8):
            tt(t1_, pr, pr, ALU.mult, rd, wr)
            tt(t2_, pi_, pi_, ALU.mult, rd, wr)
            tt(t1_, t1_, t2_, ALU.subtract, rd, wr)
            stt(t2_, pr, 1.0, pi_, ALU.add, ALU.mult, rd, wr)
            stt(pr, pr, 2.0, t1_, ALU.mult, ALU.add, rd, wr)
            ts(pi_, t2_, 2.0, None, ALU.mult, None, rd, wr)
        nr, ni, den = pr, pi_, T0[4]
        T0 = list(T0)
        T0[7], T0[8] = T0[5], T0[9]
        lim = P[:, P_LIM:P_LIM + 16]
        tt(den, lre, lre, ALU.mult, rd2, wr)
        tt(T0[7], lim, lim, ALU.mult, rd2, wr)
        tt(den, den, T0[7], ALU.add, rd2, wr)
        recip(den, den, rd2, wr)
        tt(T0[7], nr, lre, ALU.mult, rd2, wr)
        tt(T0[8], ni, lim, ALU.mult, rd2, wr)
        tt(T0[7], T0[7], T0[8], ALU.add, rd2, wr)
        tt(col(V_CA), T0[7], den, ALU.mult, rd2, wr)
        tt(T0[7], ni, lre, ALU.mult, rd2, wr)
        tt(T0[8], nr, lim, ALU.mult, rd2, wr)
        tt(T0[7], T0[7], T0[8], ALU.subtract, rd2, wr)
        tt(col(V_CB), T0[7], den, ALU.mult, rd2, wr)
        tt(T0[7], col(V_CA), col(V_CA), ALU.mult, rd2, wr)
        tt(T0[8], col(V_CB), col(V_CB), ALU.mult, rd2, wr)
        tt(T0[7], T0[7], T0[8], ALU.add, rd2, wr)
        recip(col(V_RN2), T0[7], rd2, wr)
        scr = arena[:, 0:8192].bitcast(F32)
        scr_t = Tok()
        cmr = scr[:, 0:2048].rearrange("p (i c) -> p i c", i=16)
        cmi = scr[:, 2048:4096].rearrange("p (i c) -> p i c", i=16)
        dma("sp", scr[:, 0:4096], cmats_d, writes=[scr_t])
        ca_b = bc(col(V_CA), [128, 16, 128], 2)
        cb_b = bc(col(V_CB), [128, 16, 128], 2)
        t1 = lt[4].rearrange("p (i c) -> p i c", i=4)
        t2 = lt[5].rearrange("p (i c) -> p i c", i=4)
        for q4 in range(4):
            isl = slice(4 * q4, 4 * q4 + 4)
            cab = bc(V[:, V_CA + 4 * q4:V_CA + 4 * q4 + 4], [128, 4, 128], 2)
            cbb = bc(V[:, V_CB + 4 * q4:V_CB + 4 * q4 + 4], [128, 4, 128], 2)
            tt(t1, cmr[:, isl, :], cab, ALU.mult, [scr_t, pvv_t], [lt_t[4]])
            tt(t2, cmi[:, isl, :], cbb, ALU.mult, [scr_t, pvv_t], [lt_t[5]])
            tt(cre[:, isl, :], t1, t2, ALU.subtract, [lt_t[4], lt_t[5]], [cm_t])
            tt(t1, cmr[:, isl, :], cbb, ALU.mult, [scr_t, pvv_t], [lt_t[4]])
            tt(t2, cmi[:, isl, :], cab, ALU.mult, [scr_t, pvv_t], [lt_t[5]])
            stt(ncim[:, isl, :], t1, -1.0, t2, ALU.mult, ALU.subtract, [lt_t[4], lt_t[5]], [cm_t])
        alias_barrier(ffn_toks, [scr_t])
        alias_barrier(L0_toks, [scr_t])
        ts(ones128[:], cst[:, C_ONES:C_ONES + 128], 1.0 / 128.0, None, ALU.mult, None, [cst_t], [ones_t])
        cp(maskP[:], cst[:, C_MASKP:C_MASKP + 128], [cst_t], [mask_t])
        cp(maskS[:], cst[:, C_MASKS:C_MASKS + 128], [cst_t], [mask_t])

    prep_layer0()

    def mixer0(NT, kind, seq, g):
        alias_barrier(L0_toks, ffn_toks + L1_toks_ref)
        nh = (NT + 1) // 2
        for half in range(nh):
            a0 = 2 * half
            nt = min(2, NT - a0)
            mixer0_half(nt, a0, kind, seq, g, first=(kind == "p" and g == 0 and half == 0),
                        last=(kind == "s") or (kind == "p" and g == 3 and half == nh - 1))
        alias_barrier(ffn_toks, L0_toks)

    def mixer0_half(NT, a0, kind, seq, g, first, last):
        T = NT * 128
        c0 = a0 * 128
        hsl = slice(c0, c0 + T)
        hTt = hT_t[a0:a0 + NT]
        L = 64 if kind == "p" else 32
        nb = 128 // L
        nseg = 1 if kind == "p" else 4
        Ls = 128 // nseg
        mask = maskP if kind == "p" else maskS
        bmcol = C_BMP if kind == "p" else C_BMS
        rst = cst[:, C_RSTP:C_RSTP + T] if kind == "p" else cst[:, C_RSTS:C_RSTS + 128]
        V = pvv
        if first:
            for h in range(4):
                memset(Sst[:, h, :], 0.0, [Sst_t[h]])
                memset(Sbf[:, h, :], 0.0, [Sbf_t[h]])
            memset(car[:], 0.0, [car_t])
        if kind == "s":
            xin = lt[0][:, 0:128].rearrange("p (a i b) -> p a i b", a=2, i=16)
            dma("sp", lt[0][:, 0:128], s5x0_d, writes=[lt_t[0]])
            ca4 = bc(V[:, V_CA:V_CA + 16], [128, 16, 4], 2)
            cb4 = bc(V[:, V_CB:V_CB + 16], [128, 16, 4], 2)
            rn4 = bc(V[:, V_RN2:V_RN2 + 16], [128, 16, 4], 2)
            ta = lt[1][:, 0:64].rearrange("p (i b) -> p i b", i=16)
            tb = lt[1][:, 64:128].rearrange("p (i b) -> p i b", i=16)
            tt(ta, xin[:, 0], ca4, ALU.mult, [lt_t[0], pvv_t], [lt_t[1]])
            tt(tb, xin[:, 1], cb4, ALU.mult, [lt_t[0], pvv_t], [lt_t[1]])
            tt(ta, ta, tb, ALU.add, [lt_t[1]], [lt_t[1]])
            tt(x0p[:, 0], ta, rn4, ALU.mult, [lt_t[1], pvv_t], [x0p_t])
            tt(ta, xin[:, 1], ca4, ALU.mult, [lt_t[0], pvv_t], [lt_t[1]])
            tt(tb, xin[:, 0], cb4, ALU.mult, [lt_t[0], pvv_t], [lt_t[1]])
            tt(ta, ta, tb, ALU.subtract, [lt_t[1]], [lt_t[1]])
            tt(x0p[:, 1], ta, rn4, ALU.mult, [lt_t[1], pvv_t], [x0p_t])
        uq, uq_t = load_unit(U_ABIN + 0)
        uf, uf_t = load_unit(U_ABIN + 1)
        uqv = uq[:].rearrange("p (q k f) -> p q k f", q=4, k=8)
        ufv = uf[:].rearrange("p (q k f) -> p q k f", q=4, k=8)
        nblk = T // L
        for h in range(4):
            bq, bf = pbank(), pbank()
            for k in range(8):
                mm(ps[bq][:, 0:T], uqv[:, h, k, :], hT[:, k, hsl], k == 0, k == 7, [uq_t] + hTt, [ps_t[bq]])
            for k in range(8):
                mm(ps[bf][:, 0:T], ufv[:, h, k, :], hT[:, k, hsl], k == 0, k == 7, [uf_t] + hTt, [ps_t[bf]])
            A_, B_, C_, D_ = [t[:, 0:T] for t in ht]
            act(A_, ps[bf][:, 0:T], AF.Sigmoid, [ps_t[bf]], [ht_t[0]])
            act(B_, ps[bf][:, 0:T], AF.Sigmoid, [ps_t[bf]], [ht_t[1]], scale=-1.0)
            ts(A_, A_, V[:, V_OML + h:V_OML + h + 1], V[:, V_LB + h:V_LB + h + 1], ALU.mult, ALU.add,
               [ht_t[0], pvv_t], [ht_t[0]])
            act(A_, A_, AF.Ln, [ht_t[0]], [ht_t[0]])
            scan(C_, rst, A_, 0.0, [ht_t[0], cst_t], [ht_t[2]])
            act(D_, C_, AF.Exp, [ht_t[2]], [ht_t[3]])
            tt(qs[:, h, 0:T], ps[bq][:, 0:T], D_, ALU.mult, [ps_t[bq], ht_t[3]], [qk_t[h]])
            act(A_, C_, AF.Exp, [ht_t[2]], [ht_t[0]], scale=-1.0)
            stt(ks[:, h, 0:T], B_, V[:, V_OML + h:V_OML + h + 1], A_, ALU.mult, ALU.mult,
                [ht_t[1], ht_t[0], pvv_t], [qk_t[h]])
            cp(eend[:, h, 0:nblk], D_.rearrange("p (b l) -> p b l", b=nblk)[:, :, L - 1], [ht_t[3]], [eend_t])
        ug, ug_t = load_unit(U_ABIN + 2)
        ugv = ug[:].rearrange("p (q k f) -> p q k f", q=4, k=8)
        for h in range(4):
            b = pbank()
            for k in range(8):
                mm(ps[b][:, 0:T], ugv[:, h, k, :], hT[:, k, hsl], k == 0, k == 7, [ug_t] + hTt, [ps_t[b]])
            act(gateT[:, h, 0:T], ps[b][:, 0:T], AF.Sigmoid, [ps_t[b]], [gate_t[h]])
        uu, uu_t = load_unit(U_ABIN + 3)
        uuv = uu[:].rearrange("p (q k f) -> p q k f", q=4, k=8)
        for c in range(4):
            b = pbank()
            for k in range(8):
                mm(ps[b][:, 0:T], uuv[:, c, k, :], hT[:, k, hsl], k == 0, k == 7, [uu_t] + hTt, [ps_t[b]])
            act(uT[:, c, 0:T], ps[b][:, 0:T], AF.Copy, [ps_t[b]], [uT_t[c]])
        ui, ui_t = load_unit(U_ABIN + 4)
        uiv = ui[:].rearrange("p (k n) -> p k n", k=8)
        for a in range(NT):
            b = pbank()
            for k in range(8):
                mm(ps[b][:, :], hT[:, k, c0 + a * 128:c0 + (a + 1) * 128], uiv[:, k, :], k == 0, k == 7,
                   [ui_t, hT_t[a0 + a]], [ps_t[b]])
            act(ivb[:, a, :], ps[b][:, :], AF.Copy, [ps_t[b]], [iv_t[a]])

        if stop == "m0p":
            return
        def hgrn_tile(a):
            tsl = slice(a * 128, (a + 1) * 128)
            for h in range(4):
                hs = slice(h * 128, (h + 1) * 128)
                b1 = pbank5()
                mm(ps[b1][:, 0:128], ks[:, h, tsl], qs[:, h, tsl], True, True, [qk_t[h]], [ps_t[b1]])
                attm, attm_t = nsmb()
                tt(attm, ps[b1][:, 0:128], mask[:], ALU.mult, [ps_t[b1], mask_t], [attm_t])
                ck(1)
                kht, kht_t = nsmb()
                tt(kht.rearrange("p (b l) -> p b l", b=nb), ks[:, h, tsl].rearrange("p (b l) -> p b l", b=nb),
                   bc(eend[:, h, a * nb:(a + 1) * nb], [128, nb, L], 2), ALU.mult, [qk_t[h], eend_t], [kht_t])
                pv = ps[b1][:].bitcast(BF16)
                tr(pv[:, 512:640], kht, [kht_t], [ps_t[b1]])
                for i in range(nb):
                    ts(khm[:, i, :], pv[:, 512:640], cst[:, bmcol + i:bmcol + i + 1], None, ALU.mult, None,
                       [ps_t[b1], cst_t], [khm_t])
                ck(2)
                b2 = 7
                pcs = slice(h * 128, (h + 1) * 128)
                ck(3)
                for i in range(nb):
                    csl = slice(a * 128 + i * L, a * 128 + (i + 1) * L)
                    if kind == "s":
                        dma("sp", Sst[:, h, :], hs0_d[i, h], writes=[Sst_t[h]])
                        cp(Sbf[:, h, :], Sst[:, h, :], [Sst_t[h]], [Sbf_t[h]])
                    mm(ps[b2][:, h * 128 + i * L:h * 128 + (i + 1) * L], ivb[:, a, hs], attm[:, i * L:(i + 1) * L], True, False,
                       [iv_t[a], attm_t], [ps_t[b2]])
                    mm(ps[b2][:, h * 128 + i * L:h * 128 + (i + 1) * L], Sbf[:, h, :], qs[:, h, csl], False, True,
                       [Sbf_t[h], qk_t[h]], [ps_t[b2]])
                    b3 = pbank5()
                    mm(ps[b3][:, 0:128], khm[:, i, :], ivb[:, a, hs], True, True, [khm_t, iv_t[a]], [ps_t[b3]])
                    blk = a * nb + i
                    stt(Sst[:, h, :], Sst[:, h, :], eend[:, h, blk:blk + 1], ps[b3][:, 0:128], ALU.mult, ALU.add,
                        [Sst_t[h], eend_t, ps_t[b3]], [Sst_t[h]])
                    ck(4)
                    if kind == "s":
                        dma("sp", o_hgs[i, h], Sst[:, h, :], reads=[Sst_t[h]])
                    else:
                        act(Sbf[:, h, :], Sst[:, h, :], AF.Copy, [Sst_t[h]], [Sbf_t[h]])
                yield
            for h in range(4):
                b2 = 7
                pcs = slice(h * 128, (h + 1) * 128)
                osq, osq_t = nsmb()
                act(osq, ps[b2][:, pcs], AF.Square, [ps_t[b2]], [osq_t])
                b4 = pbank5()
                mm(ps[b4][:, 0:128], ones128[:], osq, True, True, [ones_t, osq_t], [ps_t[b4]])
                sd, sd_t = nsm()
                act(sd, ps[b4][:, 0:128], AF.Sqrt, [ps_t[b4]], [sd_t], bias=EPS)
                recip(sd, sd, [sd_t], [sd_t])
                y1, y1_t = nsm()
                tt(y1, ps[b2][:, pcs], sd, ALU.mult, [ps_t[b2], sd_t], [y1_t])
                stt(abT[:, h, tsl], y1, prm[:, P_OUTG:P_OUTG + 1], gateT[:, h, tsl], ALU.mult, ALU.mult,
                    [y1_t, prm_t, gate_t[h]], [abT_t[h]])
                yield

        if stop == "m0h":
            return
        wglu = wsm[:, W_GLU:W_GLU + 2048].rearrange("p (c o f) -> p c o f", c=4, o=4)
        ubt, ubt_t = load_unit(U_BT)
        btre = ubt[:, 0:2048].rearrange("p (i s) -> p i s", i=16)
        btim = ubt[:, 2048:4096].rearrange("p (i s) -> p i s", i=16)

        def v4(ap512):
            return ap512.rearrange("p (i s j) -> p i s j", i=4, s=nseg)

        def s5_tile(a):
            tsl = slice(a * 128, (a + 1) * 128)
            py = 4
            def issue_bu(c):
                u_b, u_bt = ubs[c % 2], ubs_t[c % 2]
                act(u_b, uT[:, c, tsl], AF.Copy, [uT_t[c]], [u_bt])
                br_, bi_ = pbu(), pbu()
                for il in range(4):
                    mm(ps[br_][:, il * 128:(il + 1) * 128], btre[:, 4 * c + il, :], u_b, True, True, [ubt_t, u_bt], [ps_t[br_]])
                    mm(ps[bi_][:, il * 128:(il + 1) * 128], btim[:, 4 * c + il, :], u_b, True, True, [ubt_t, u_bt], [ps_t[bi_]])
                return br_, bi_

            nxt = issue_bu(0)
            for c in range(4):
                isl = slice(4 * c, 4 * c + 4)
                br_, bi_ = nxt
                if c + 1 < 4:
                    nxt = issue_bu(c + 1)
                cs4 = cosT[:, isl, 0:Ls].unsqueeze(2).to_broadcast([128, 4, nseg, Ls])
                sn4 = sinT[:, isl, 0:Ls].unsqueeze(2).to_broadcast([128, 4, nseg, Ls])
                t1, t1_t = nlt(); t2, t2_t = nlt(); wr_, wr_t = nlt(); wi_, wi_t = nlt()
                zr, zr_t = nlt(); zi, zi_t = nlt()
                rdt = [tab_t]
                tt(v4(t1), v4(ps[br_][:, :]), cs4, ALU.mult, [ps_t[br_]] + rdt, [t1_t])
                tt(v4(t2), v4(ps[bi_][:, :]), sn4, ALU.mult, [ps_t[bi_]] + rdt, [t2_t])
                tt(v4(zr), v4(ps[bi_][:, :]), cs4, ALU.mult, [ps_t[bi_]] + rdt, [zr_t])
                tt(v4(zi), v4(ps[br_][:, :]), sn4, ALU.mult, [ps_t[br_]] + rdt, [zi_t])
                tt(wr_, t1, t2, ALU.add, [t1_t, t2_t], [wr_t])
                tt(wi_, zr, zi, ALU.subtract, [zr_t, zi_t], [wi_t])
                for il in range(4):
                    i = 4 * c + il
                    for sgi in range(nseg):
                        sl = slice(il * 128 + sgi * Ls, il * 128 + (sgi + 1) * Ls)
                        if kind == "p":
                            ir, ii, it = car[:, 0, i, 0:1], car[:, 1, i, 0:1], car_t
                        else:
                            ir, ii, it = x0p[:, 0, i, sgi:sgi + 1], x0p[:, 1, i, sgi:sgi + 1], x0p_t
                        rbc = V[:, V_R + i:V_R + i + 1].to_broadcast([128, Ls])
                        scan(zr[:, sl], rbc, wr_[:, sl], ir, [wr_t, it, pvv_t], [zr_t])
                        scan(zi[:, sl], rbc, wi_[:, sl], ii, [wi_t, it, pvv_t], [zi_t])
                zre = v4(zr)[:, :, :, Ls - 1]
                zie = v4(zi)[:, :, :, Ls - 1]
                cL = bc(cosT[:, isl, Ls - 1], [128, 4, nseg], 2)
                sL = bc(sinT[:, isl, Ls - 1], [128, 4, nseg], 2)
                e1 = t1[:, 0:4 * nseg].rearrange("p (i s) -> p i s", i=4)
                e2 = t2[:, 0:4 * nseg].rearrange("p (i s) -> p i s", i=4)
                e3 = t1[:, 64:64 + 4 * nseg].rearrange("p (i s) -> p i s", i=4)
                e4 = t2[:, 64:64 + 4 * nseg].rearrange("p (i s) -> p i s", i=4)
                tt(e1, zre, cL, ALU.mult, [zr_t, tab_t], [t1_t])
                tt(e2, zie, sL, ALU.mult, [zi_t, tab_t], [t2_t])
                tt(e3, zre, sL, ALU.mult, [zr_t, tab_t], [t1_t])
                tt(e4, zie, cL, ALU.mult, [zi_t, tab_t], [t2_t])
                tt(car[:, 0, isl, 0:nseg], e1, e2, ALU.subtract, [t1_t, t2_t], [car_t])
                tt(car[:, 1, isl, 0:nseg], e3, e4, ALU.add, [t1_t, t2_t], [car_t])
                tt(v4(t1), v4(zr), cs4, ALU.mult, [zr_t, tab_t], [t1_t])
                tt(v4(t2), v4(zi), sn4, ALU.mult, [zi_t, tab_t], [t2_t])
                tt(v4(wr_), v4(zr), sn4, ALU.mult, [zr_t, tab_t], [wr_t])
                tt(v4(wi_), v4(zi), cs4, ALU.mult, [zi_t, tab_t], [wi_t])
                tt(xrb, t1, t2, ALU.subtract, [t1_t, t2_t], [xb_t])
                tt(xib, wr_, wi_, ALU.add, [wr_t, wi_t], [xb_t])
                for il in range(4):
                    i = 4 * c + il
                    mm(ps[py][:, c * 128:(c + 1) * 128], cre[:, i, :], xrb[:, il * 128:(il + 1) * 128], il == 0, False,
                       [cm_t, xb_t], [ps_t[py]])
                    mm(ps[py][:, c * 128:(c + 1) * 128], ncim[:, i, :], xib[:, il * 128:(il + 1) * 128], False, il == 3,
                       [cm_t, xb_t], [ps_t[py]])
                yield
            yd, yd_t = nlt()
            y2, y2_t = nlt()
            yd4 = yd.rearrange("p (c t) -> p c t", c=4)
            tt(yd4, uT[:, :, tsl], bc(prm[:, P_SD:P_SD + 4], [128, 4, 128], 2), ALU.mult, list(uT_t) + [prm_t], [yd_t])
            tt(yd, yd, ps[py][:, :], ALU.add, [yd_t, ps_t[py]], [yd_t])
            tt(y2, yd, yd, ALU.mult, [yd_t], [y2_t])
            ts(y2, y2, 0.044715, 1.0, ALU.mult, ALU.add, [y2_t], [y2_t])
            tt(y2, y2, yd, ALU.mult, [y2_t, yd_t], [y2_t])
            act(y2, y2, AF.Sigmoid, [y2_t], [y2_t], scale=1.5957691216057308)
            tt(zf.rearrange("p c t -> p (c t)"), yd, y2, ALU.mult, [yd_t, y2_t], list(z_t))
            act(zb.rearrange("p c t -> p (c t)"), zf.rearrange("p c t -> p (c t)"), AF.Copy, list(z_t), list(z_t))
            for fo in range(4):
                pg = pbank5()
                for c in range(4):
                    mm(ps[pg][:, 0:128], wglu[:, c, fo, :], zb[:, c, :], c == 0, c == 3, [wsm_t, z_t[c]], [ps_t[pg]])
                s2, s2_t = nsm()
                act(s2, ps[pg][:, 0:128], AF.Sigmoid, [ps_t[pg], prm_t], [s2_t], bias=prm[:, P_BGLU + fo:P_BGLU + fo + 1])
                tt(abT[:, 4 + fo, tsl], zf[:, fo, :], s2, ALU.mult, [z_t[fo], s2_t], [abT_t[4 + fo]])

            yield

        for a in range(NT):
            gens = [s5_tile(a), hgrn_tile(a)]
            while gens:
                for gen in list(gens):
                    try:
                        next(gen)
                    except StopIteration:
                        gens.remove(gen)
        if stop == "m0s":
            return
        out_proj(NT, a0, U_ABOUT, abT, abT_t)

        if last:
            if kind == "p":
                for h in range(4):
                    dma("sp", o_hgp[seq, h], Sst[:, h, :], reads=[Sst_t[h]])
            ca4 = bc(V[:, V_CA:V_CA + 16], [128, 16, nseg], 2)
            cb4 = bc(V[:, V_CB:V_CB + 16], [128, 16, nseg], 2)
            fin, fin_t = nlt()
            memset(fin[:, 0:128], 0.0, [fin_t])
            fr = fin[:, 0:64].rearrange("p (s i) -> p i s", s=4)[:, :, 0:nseg]
            fi = fin[:, 64:128].rearrange("p (s i) -> p i s", s=4)[:, :, 0:nseg]
            t1, t1_t = nlt()
            ea = t1[:, 0:16 * nseg].rearrange("p (i s) -> p i s", i=16)
            eb = t1[:, 64:64 + 16 * nseg].rearrange("p (i s) -> p i s", i=16)
            cr_, ci_ = car[:, 0, :, 0:nseg], car[:, 1, :, 0:nseg]
            tt(ea, cr_, ca4, ALU.mult, [car_t, pvv_t], [t1_t])
            tt(eb, ci_, cb4, ALU.mult, [car_t, pvv_t], [t1_t])
            tt(fr, ea, eb, ALU.subtract, [t1_t], [fin_t])
            tt(ea, ci_, ca4, ALU.mult, [car_t, pvv_t], [t1_t])
            tt(eb, cr_, cb4, ALU.mult, [car_t, pvv_t], [t1_t])
            tt(fi, ea, eb, ALU.add, [t1_t], [fin_t])
            pt = pbank()
            tr(ps[pt][:, 0:128], fin[:, 0:128], [fin_t], [ps_t[pt]], idn=ident32)
            ftr, ftr_t = nsm()
            cp(ftr, ps[pt][:, 0:128], [ps_t[pt]], [ftr_t])
            for part in range(2):
                for s_ in range(nseg):
                    row = part * 64 + s_ * 16
                    if kind == "p":
                        dma("sp", o_s5p[seq, part], ftr[row:row + 16, :], reads=[ftr_t])
                    else:
                        dma("sp", o_s5s[s_, part], ftr[row:row + 16, :], reads=[ftr_t])

    eend = sb("eend", [128, 4, 8]); eend_t = Tok()

    def run_group(kind, seq, g):
        NT = 4 if kind == "p" else 1
        for a in range(NT):
            src = xp[seq, (4 * g + a) * 128:(4 * g + a + 1) * 128, :] if kind == "p" else xs[:, :]
            dma("sp", x[:, a, :], src, writes=[x_t[a]])
        for l in range(2):
            rmsnorm_to_hT(NT, P_GAIN + (3 * l + 0) * 8)
            ffn(NT, U_FFN + (2 * l + 0) * 17)
            if stop == "ffn1_%d" % l:
                break
            rmsnorm_to_hT(NT, P_GAIN + (3 * l + 1) * 8)
            if l == 0:
                mixer0(NT, kind, seq, g)
            else:
                mixer1(NT, kind, seq, g)
            if stop == "mix_%d" % l or stop in ("pg", "sg", "m0p", "m0h", "m0s"):
                break
            if l == 1 and stop in ("pg1", "sg1"):
                break
            rmsnorm_to_hT(NT, P_GAIN + (3 * l + 2) * 8)
            ffn(NT, U_FFN + (2 * l + 1) * 17)
            if stop == "ffn2_%d" % l:
                break
        for a in range(NT):
            dst = yp[seq, (4 * g + a) * 128:(4 * g + a + 1) * 128, :] if kind == "p" else ys[:, :]
            dma("sp", dst, x[:, a, :], reads=[x_t[a]])

    KT = sb("KT", [128, 8, 2048], BF16); KT_t = toks(16)
    Vv = sb("Vv", [128, 16, 512], BF16); Vv_t = toks(16)
    wsTp = sb("wsTp", [128, 4, 128], BF16); wsTs = sb("wsTs", [128, 4, 128], BF16); wsT_t = Tok()
    gain1 = sb("gain1", [128, 192]); gain1_t = Tok()
    amask = sb("amask", [128, 128], BF16); smask = sb("smask", [128, 128], BF16); ones64 = sb("ones64", [128, 64], BF16)
    am_t = Tok()
    arena_off[0] = 0
    scratch_toks.clear()
    ugT, ug1_t = carve([128, 4, MT], BF16, 4)
    vnb, vnb_t = carve([128, 2, 512], BF16, 2)
    QT, QT_t = carve([128, 8, MT], BF16, 2)
    cdT, cdT_t = carve([128, 8, MT], BF16, 8)
    g1 = []
    g1_t = []
    for _i in range(4):
        _v, _t = carve([128, 512]); g1.append(_v); g1_t.append(_t)
    g1_rr = [0]
    qsb, qsb_t = carve([128, 8, 96])
    qsq, qsq_t = carve([128, 8, 96])
    qnb, qnb_t = carve([128, 8, 96], BF16)
    ksb, ksb_t = carve([128, 8, 96])
    cqn, cqn_t = carve([128, 256], BF16)
    cqT, cqT_t = carve([128, 2, 128], BF16)
    ckvf, ckvf_t = carve([128, 128])
    ckvb, ckvb_t = carve([128, 128], BF16)
    ckvT, ckvT_t = carve([128, 128], BF16)
    kper, kper_t = carve([128, 32])
    rt16 = []
    rt16_t = []
    for _i in range(2):
        _v, _t = carve([128, 8, 16]); rt16.append(_v); rt16_t.append(_t)
    PT = []
    PT_t = []
    for _i in range(3):
        _v, _t = carve([128, MT], BF16); PT.append(_v); PT_t.append(_t)
    PT_rr = [0]
    rden, rden_t = carve([128, MT])
    pck, pck_t = carve([128, 128])
    L1_toks = list(scratch_toks)
    L1_toks_ref.extend(L1_toks)

    def ng1():
        i = g1_rr[0] % 4
        g1_rr[0] += 1
        return g1[i], g1_t[i]

    def nPT():
        i = PT_rr[0] % 3
        PT_rr[0] += 1
        return PT[i], PT_t[i]

    def prep_layer1():
        ts(gain1[:, 0:96], prm[:, P_QGAIN:P_QGAIN + 96], float(96.0 ** -0.5), None, ALU.mult, None, [prm_t], [gain1_t])
        cp(gain1[:, 96:192], prm[:, P_KGAIN:P_KGAIN + 96], [prm_t], [gain1_t])
        cp(amask[:], cst[:, C_AMASK:C_AMASK + 128], [cst_t], [am_t])
        cp(smask[:], cst[:, C_SMASK:C_SMASK + 128], [cst_t], [am_t])
        cp(ones64[:], cst[:, C_ONES:C_ONES + 64], [cst_t], [am_t])
        gw, gw_t = g1[0], g1_t[0]
        gwv = gw.rearrange("p (g s) -> p g s", g=4)
        dma("sp", gw, gws_d, writes=[gw_t])
        gm, gm_t = g1[1], g1_t[1]
        gmb = gm.bitcast(BF16)[:, 0:512].rearrange("p (g s) -> p g s", g=4)
        tt(gmb, gwv, bc(cst[:, C_GTRILP:C_GTRILP + 128], [128, 4, 128], 1), ALU.mult, [gw_t, cst_t], [gm_t])
        b = pbank(True)
        pv = ps[b][:].bitcast(BF16)
        for gg in range(4):
            tr(pv[:, gg * 128:(gg + 1) * 128], gmb[:, gg, :], [gm_t], [ps_t[b]])
        cp(wsTp[:], pv[:, 0:512].rearrange("p (g t) -> p g t", g=4), [ps_t[b]], [wsT_t])
        gs, gs_t = g1[2], g1_t[2]
        memset(gs, 0.0, [gs_t])
        gsv = gs.rearrange("p (g s) -> p g s", g=4)
        gsrc = gws_d.rearrange("p (g s) -> p g s", g=4)
        for bb in range(4):
            dma("sp", gsv[32 * bb:32 * bb + 32, :, 32 * bb:32 * bb + 32], gsrc[0:32, :, 0:32], writes=[gs_t])
        gm2, gm2_t = g1[3], g1_t[3]
        gm2b = gm2.bitcast(BF16)[:, 0:512].rearrange("p (g s) -> p g s", g=4)
        tt(gm2b, gsv, bc(cst[:, C_GTRILS:C_GTRILS + 128], [128, 4, 128], 1), ALU.mult, [gs_t, cst_t], [gm2_t])
        b = pbank(True)
        pv = ps[b][:].bitcast(BF16)
        for gg in range(4):
            tr(pv[:, gg * 128:(gg + 1) * 128], gm2b[:, gg, :], [gm2_t], [ps_t[b]])
        cp(wsTs[:], pv[:, 0:512].rearrange("p (g t) -> p g t", g=4), [ps_t[b]], [wsT_t])

    alias_barrier(L1_toks, L0_toks + ffn_toks)
    prep_layer1()
    alias_barrier(L0_toks + ffn_toks, L1_toks)

    wuq = wsm[:, W_UQ:W_UQ + 1536].rearrange("p (k n) -> p k n", k=2)
    wukv = wsm[:, W_UKV:W_UKV + 1024]

    def gelu_from(src_ap, src_toks, out_ap, out_toks, n):
        yd, yd_t = ng1()
        y2, y2_t = ng1()
        act(yd[:, 0:n], src_ap, AF.Copy, src_toks, [yd_t])
        tt(y2[:, 0:n], yd[:, 0:n], yd[:, 0:n], ALU.mult, [yd_t], [y2_t])
        ts(y2[:, 0:n], y2[:, 0:n], 0.044715, 1.0, ALU.mult, ALU.add, [y2_t], [y2_t])
        tt(y2[:, 0:n], y2[:, 0:n], yd[:, 0:n], ALU.mult, [y2_t, yd_t], [y2_t])
        act(y2[:, 0:n], y2[:, 0:n], AF.Sigmoid, [y2_t], [y2_t], scale=1.5957691216057308)
        tt(out_ap, yd[:, 0:n], y2[:, 0:n], ALU.mult, [yd_t, y2_t], out_toks)

    def rms_stat(src_ap, src_toks, n, junk_ap, junk_toks):
        st, st_t = new_stat()
        act(junk_ap, src_ap, AF.Square, src_toks, list(junk_toks) + [st_t], accum=st[:, 0:1])
        act(st[:, 1:2], st[:, 0:1], AF.Sqrt, [st_t], [st_t], scale=1.0 / n, bias=EPS)
        recip(st[:, 2:3], st[:, 1:2], [st_t], [st_t])
        return st[:, 2:3], st_t

    def head_norm_T(srcsb, srcsb_t, gcol, dst, dst_tok, dcol):
        tt(qsq, srcsb, srcsb, ALU.mult, [srcsb_t], [qsq_t])
        ss, ss_t = rt16[0][:, :, 0], rt16_t[0]
        reduce_sum(ss, qsq, [qsq_t], [ss_t])
        act(ss, ss, AF.Sqrt, [ss_t], [ss_t], scale=1.0 / 96.0, bias=EPS)
        recip(ss, ss, [ss_t], [ss_t])
        tt(qsq, srcsb, bc(ss, [128, 8, 96], 2), ALU.mult, [srcsb_t, ss_t], [qsq_t])
        tt(qnb, qsq, bc(gain1[:, gcol:gcol + 96], [128, 8, 96], 1), ALU.mult, [qsq_t, gain1_t], [qnb_t])
        ck(134)
        b = pbank(True)
        pv = ps[b][:].bitcast(BF16)
        for h in range(8):
            tr(pv[0:96, h * 128:(h + 1) * 128], qnb[:, h, :], [qnb_t], [ps_t[b]])
        cp(dst[0:96, :, dcol:dcol + 128], pv[0:96, :].rearrange("p (h t) -> p h t", h=8), [ps_t[b]], [dst_tok])

    def rope(x1, x2, cos_b, sin_b, rd, wr, shape3):
        n = shape3
        if len(n) == 3:
            tv = [rt16[0][:, 0:4, :], rt16[0][:, 4:8, :], rt16[1][:, 0:4, :], rt16[1][:, 4:8, :]]
        else:
            tv = [rt16[0][:, 0, :], rt16[0][:, 4, :], rt16[1][:, 0, :], rt16[1][:, 4, :]]
        tk = [rt16_t[0], rt16_t[0], rt16_t[1], rt16_t[1]]
        tt(tv[0], x1, cos_b, ALU.mult, rd, [tk[0]])
        tt(tv[1], x2, sin_b, ALU.mult, rd, [tk[1]])
        tt(tv[2], x1, sin_b, ALU.mult, rd, [tk[2]])
        tt(tv[3], x2, cos_b, ALU.mult, rd, [tk[3]])
        tt(x1, tv[0], tv[1], ALU.subtract, [tk[0]], wr)
        tt(x2, tv[2], tv[3], ALU.add, [tk[2]], wr)

    def build_kv(slot):
        b = pbank(True)
        pv = ps[b][:].bitcast(BF16)
        tr(pv[:, 0:128], ckvb, [ckvb_t], [ps_t[b]])
        cp(ckvT, pv[:, 0:128], [ps_t[b]], [ckvT_t])
        for hfk in range(2):
            b = pbank(True)
            mm(ps[b][:, :], ckvT, wukv[:, hfk * 512:(hfk + 1) * 512], True, True, [ckvT_t, wsm_t], [ps_t[b]])
            kvv = ps[b][:, :].rearrange("p (h d) -> p h d", h=4)
            act(Vv[:, slot, hfk * 256:(hfk + 1) * 256].rearrange("p (h d) -> p h d", h=4), kvv[:, :, 64:128], AF.Copy,
                [ps_t[b]], [Vv_t[slot]])
            act(ksb[:, 4 * hfk:4 * hfk + 4, 0:64], kvv[:, :, 0:64], AF.Copy, [ps_t[b]], [ksb_t])
        cp(ksb[:, :, 64:96], bc(kper, [128, 8, 32], 1), [kper_t], [ksb_t])
        head_norm_T(ksb, ksb_t, 96, KT, KT_t[slot], slot * 128)

    def mixer1(NT, kind, seq, g):
        alias_barrier(L1_toks, ffn_toks + L0_toks)
        nh = (NT + 1) // 2
        for half in range(nh):
            a0 = 2 * half
            nt = min(2, NT - a0)
            mixer1_half(nt, a0, kind, seq, g)
        alias_barrier(ffn_toks + L0_toks, L1_toks)

    def mixer1_half(NT, a0, kind, seq, g):
        T = NT * 128
        c0 = a0 * 128
        hsl = slice(c0, c0 + T)
        hTt = hT_t[a0:a0 + NT]
        gbase = 4 * g + a0 if kind == "p" else 15
        wsT = wsTp if kind == "p" else wsTs
        bscol = P_BS if kind == "p" else P_BSS
        uu, uu_t = load_unit(U_CDIN + 0)
        uuv = uu[:].rearrange("p (q k f) -> p q k f", q=4, k=8)
        for cg in range(4):
            b = pbank(True)
            for k in range(8):
                mm(ps[b][:, 0:T], uuv[:, cg, k, :], hT[:, k, hsl], k == 0, k == 7, [uu_t] + hTt, [ps_t[b]])
            gelu_from(ps[b][:, 0:T], [ps_t[b]], ugT[:, cg, 0:T], [ug1_t[cg]], T)
        ck(11)
        uv, uv_t = load_unit(U_CDIN + 1)
        uvv = uv[:].rearrange("p (k n) -> p k n", k=8)
        for a in range(NT):
            b = pbank(True)
            for k in range(8):
                mm(ps[b][:, :], hT[:, k, c0 + a * 128:c0 + (a + 1) * 128], uvv[:, k, :], k == 0, k == 7,
                   [uv_t, hT_t[a0 + a]], [ps_t[b]])
            vg, vg_t = ng1()
            gelu_from(ps[b][:, :], [ps_t[b]], vg, [vg_t], 512)
            jk, jk_t = ng1()
            rs, rs_t = rms_stat(vg, [vg_t], 512, jk, [jk_t])
            vnf, vnf_t = ng1()
            stt(vnf, vg, rs, prm[:, P_VNORM:P_VNORM + 512], ALU.mult, ALU.mult, [vg_t, rs_t, prm_t], [vnf_t])
            act(vnb[:, a, :], vnf, AF.Copy, [vnf_t], [vnb_t[a]])
            if kind == "s":
                dma("sp", o_gvs, vnf, reads=[vnf_t])
            elif gbase + a == 15:
                dma("sp", o_gvp[seq], vnf, reads=[vnf_t])
        ck(12)
        ur, ur_t = load_unit(U_CDIN + 2)
        urv = ur[:].rearrange("p (k n) -> p k n", k=8)
        for a in range(NT):
            gt = gbase + a
            b = pbank(True)
            for k in range(8):
                mm(ps[b][:, 0:416], hT[:, k, c0 + a * 128:c0 + (a + 1) * 128], urv[:, k, 0:416], k == 0, k == 7,
                   [ur_t, hT_t[a0 + a]], [ps_t[b]])
            if kind == "p":
                cosr = cst[:, C_ROPEP + gt * 16:C_ROPEP + gt * 16 + 16]
                sinr = cst[:, C_ROPEP + 256 + gt * 16:C_ROPEP + 256 + gt * 16 + 16]
            else:
                cosr = cst[:, C_ROPES:C_ROPES + 16]
                sinr = cst[:, C_ROPES + 16:C_ROPES + 32]
            jk, jk_t = ng1()
            rs, rs_t = rms_stat(ps[b][:, 0:256], [ps_t[b]], 256, jk[:, 0:256], [jk_t])
            stt(cqn, ps[b][:, 0:256], rs, prm[:, P_QNORM:P_QNORM + 256], ALU.mult, ALU.mult,
                [ps_t[b], rs_t, prm_t], [cqn_t])
            rs2, rs2_t = rms_stat(ps[b][:, 256:384], [ps_t[b]], 128, jk[:, 256:384], [jk_t])
            stt(ckvf, ps[b][:, 256:384], rs2, prm[:, P_KVNORM:P_KVNORM + 128], ALU.mult, ALU.mult,
                [ps_t[b], rs2_t, prm_t], [ckvf_t])
            act(ckvb, ckvf, AF.Copy, [ckvf_t], [ckvb_t])
            act(kper, ps[b][:, 384:416], AF.Copy, [ps_t[b]], [kper_t])
            rope(kper[:, 0:16], kper[:, 16:32], cosr, sinr, [kper_t, cst_t], [kper_t], (128, 16))
            if kind == "p":
                dma("sp", o_ckvp[seq, gt * 128:(gt + 1) * 128, :], ckvf, reads=[ckvf_t])
                dma("sp", o_kpep[seq, gt * 128:(gt + 1) * 128, :], kper, reads=[kper_t])
            else:
                dma("sp", o_ckvs, ckvf, reads=[ckvf_t])
                dma("sp", o_kpes, kper, reads=[kper_t])
            ck(13)
            b2 = pbank(True)
            pv2 = ps[b2][:].bitcast(BF16)
            for kc in range(2):
                tr(pv2[:, kc * 128:(kc + 1) * 128], cqn[:, kc * 128:(kc + 1) * 128], [cqn_t], [ps_t[b2]])
            cp(cqT, pv2[:, 0:256].rearrange("p (k t) -> p k t", k=2), [ps_t[b2]], [cqT_t])
            ck(131)
            for cgp in range(2):
                b3 = pbank(True)
                for kc in range(2):
                    mm(ps[b3][:, 0:384], cqT[:, kc, :], wuq[:, kc, cgp * 384:(cgp + 1) * 384], kc == 0, kc == 1,
                       [cqT_t, wsm_t], [ps_t[b3]])
                qv = ps[b3][:, 0:384].rearrange("p (h d) -> p h d", h=4)
                hs4 = slice(4 * cgp, 4 * cgp + 4)
                act(qsb[:, hs4, :], qv, AF.Copy, [ps_t[b3]], [qsb_t])
                ck(132)
                rope(qsb[:, hs4, 64:80], qsb[:, hs4, 80:96],
                     bc(cosr, [128, 4, 16], 1), bc(sinr, [128, 4, 16], 1), [qsb_t, cst_t], [qsb_t], (128, 4, 16))
            ck(133)
            head_norm_T(qsb, qsb_t, 0, QT, QT_t[a], a * 128)
            ck(14)
            build_kv(gt)
        ck(15)
        for a in range(NT):
            b = pbank(True)
            for gg in range(4):
                mm(ps[b][:, gg * 128:(gg + 1) * 128], vnb[:, a, gg * 128:(gg + 1) * 128], wsT[:, gg, :], True, True,
                   [vnb_t[a], wsT_t], [ps_t[b]])
            tmp, tmp_t = ng1()
            tt(tmp, ps[b][:, :], prm[:, bscol:bscol + 512], ALU.add, [ps_t[b], prm_t], [tmp_t])
            tt(cdT[:, 0:4, a * 128:(a + 1) * 128], tmp.rearrange("p (g t) -> p g t", g=4), ugT[:, :, a * 128:(a + 1) * 128],
               ALU.mult, [tmp_t] + list(ug1_t), list(cdT_t[0:4]))
        ck(16)
        if kind == "p":
            gt_max = gbase + NT - 1
            for hp in range(4):
                bn, bd = 4 + 2 * (hp % 2), 5 + 2 * (hp % 2)
                for hh in range(2):
                    h = 2 * hp + hh
                    rows = slice(64 * hh, 64 * hh + 64)
                    tp = (0, 64) if hh == 1 else None
                    for qa in range(NT):
                        cs_ = slice(qa * 128, (qa + 1) * 128)
                        for kt in range(gbase + qa + 1):
                            last = (kt == gbase + qa)
                            bS = pbank(True)
                            mm(ps[bS][:, 0:128], KT[0:96, h, kt * 128:(kt + 1) * 128], QT[0:96, h, cs_], True, True,
                               [KT_t[kt], QT_t[qa]], [ps_t[bS]])
                            pt_, pt_t = nPT()
                            act(pt_[:, 0:128], ps[bS][:, 0:128], AF.Exp, [ps_t[bS]], [pt_t])
                            if last:
                                tt(pt_[:, 0:128], pt_[:, 0:128], amask[:], ALU.mult, [pt_t, am_t], [pt_t])
                            mm(ps[bn][rows, cs_], Vv[:, kt, h * 64:(h + 1) * 64], pt_[:, 0:128], kt == 0, last,
                               [Vv_t[kt], pt_t], [ps_t[bn]], tp=tp)
                            mm(ps[bd][rows, cs_], ones64[:], pt_[:, 0:128], kt == 0, last,
                               [am_t, pt_t], [ps_t[bd]], tp=tp)
                recip(rden[:, 0:T], ps[bd][:, 0:T], [ps_t[bd]], [rden_t])
                tt(cdT[:, 4 + hp, 0:T], ps[bn][:, 0:T], rden[:, 0:T], ALU.mult, [ps_t[bn], rden_t], [cdT_t[4 + hp]])
        else:
            for sb_ in range(4):
                qc = slice(32 * sb_, 32 * sb_ + 32)
                for kt in range(8):
                    dma("sp", pck, pckv_d[sb_, kt * 128:(kt + 1) * 128, :], writes=[pck_t])
                    dma("sp", kper, pkpe_d[sb_, kt * 128:(kt + 1) * 128, :], writes=[kper_t])
                    act(ckvb, pck, AF.Copy, [pck_t], [ckvb_t])
                    build_kv(kt)
                for h in range(8):
                    hp, hh = h // 2, h % 2
                    bn = 4 + hp
                    rows = slice(64 * hh, 64 * hh + 64)
                    tp = (0, 64) if hh == 1 else None
                    for ki, slot in enumerate([15] + list(range(8))):
                        bS = pbank(True)
                        mm(ps[bS][:, 0:32], KT[0:96, h, slot * 128:(slot + 1) * 128], QT[0:96, h, qc], True, True,
                           [KT_t[slot], QT_t[0]], [ps_t[bS]])
                        pt_, pt_t = nPT()
                        act(pt_[:, 0:32], ps[bS][:, 0:32], AF.Exp, [ps_t[bS]], [pt_t])
                        if slot == 15:
                            tt(pt_[:, 0:32], pt_[:, 0:32], smask[:, qc], ALU.mult, [pt_t, am_t], [pt_t])
                        oc = slice(hp * 128 + 32 * sb_, hp * 128 + 32 * sb_ + 32)
                        mm(ps[4][rows, oc], Vv[:, slot, h * 64:(h + 1) * 64], pt_[:, 0:32], ki == 0, ki == 8,
                           [Vv_t[slot], pt_t], [ps_t[4]], tp=tp)
                        mm(ps[5][rows, oc], ones64[:], pt_[:, 0:32], ki == 0, ki == 8,
                           [am_t, pt_t], [ps_t[5]], tp=tp)
            for hp in range(4):
                hc = slice(hp * 128, (hp + 1) * 128)
                recip(rden[:, 0:128], ps[5][:, hc], [ps_t[5]], [rden_t])
                tt(cdT[:, 4 + hp, 0:128], ps[4][:, hc], rden[:, 0:128], ALU.mult, [ps_t[4], rden_t], [cdT_t[4 + hp]])
        ck(17)
        out_proj(NT, a0, U_CDOUT, cdT, cdT_t)


    groups = [("p", s, g) for s in range(2) for g in range(4)] + [("s", 0, 0)]
    if debug:
        groups = [("p", 0, 0), ("s", 0, 0)]
    if stop == "prep":
        groups = []
        dma("sp", ys[:, 0:128], cosT[:, 0, :], reads=[tab_t])
        dma("sp", ys[:, 128:256], sinT[:, 5, :], reads=[tab_t])
        dma("sp", ys[:, 256:512], pvv[:], reads=[pvv_t])
    if stop in ("pg", "m0p", "m0h", "m0s") or stop.startswith("ck"):
        groups = [("p", 0, 0)]
    if stop in ("sg", "sg1"):
        groups = [("s", 0, 0)]
    if stop == "pg1":
        groups = [("p", 0, 0)]
    try:
        for kind, seq, g in groups:
            run_group(kind, seq, g)
    except _Stop:
        pass

    S.finish()
    S.build()
    es.close()
    return nc


NRING = 3
U_FFN = 0
U_ABIN = 68
U_ABOUT = 73
U_CDIN = 75
U_CDOUT = 78
U_BT = 80
NUNITS = 81

C_IDENT = 0
C_MASKP = 128
C_MASKS = 256
C_BMP = 384
C_BMS = 386
C_JV = 390
C_RSTP = 518
C_RSTS = 1030
C_ONES = 1158
C_AMASK = 1286
C_SMASK = 1414
C_GTRILP = 1542
C_GTRILS = 1670
C_ROPEP = 1798
C_ROPES = 2310
NCONST = 2342

P_GAIN = 0
P_LRE = 48
P_LIM = 64
P_LOGDT = 80
P_SD = 96
P_BGLU = 100
P_LB = 104
P_OUTG = 116
P_VNORM = 117
P_QNORM = 629
P_KVNORM = 885
P_QGAIN = 1013
P_KGAIN = 1109
P_BS = 1205
P_BSS = 1717
NPARAM = 2229

W_GLU = 0
W_UQ = 2048
W_UKV = 3584
NSMALL = 4608


def host_consts():
    c = np.zeros((128, NCONST), np.float32)
    ar = np.arange(128)
    c[:, C_IDENT:C_IDENT + 128] = np.eye(128, dtype=np.float32)
    s_, t_ = ar[:, None], ar[None, :]
    c[:, C_MASKP:C_MASKP + 128] = ((s_ // 64 == t_ // 64) & (s_ <= t_)).astype(np.float32)
    c[:, C_MASKS:C_MASKS + 128] = ((s_ // 32 == t_ // 32) & (s_ <= t_)).astype(np.float32)
    c[:, C_BMP:C_BMP + 2] = (ar[:, None] // 64 == np.arange(2)[None, :]).astype(np.float32)
    c[:, C_BMS:C_BMS + 4] = (ar[:, None] // 32 == np.arange(4)[None, :]).astype(np.float32)
    c[:, C_JV:C_JV + 128] = (ar + 1).astype(np.float32)[None, :]
    c[:, C_RSTP:C_RSTP + 512] = (np.arange(512) % 64 != 0).astype(np.float32)[None, :]
    c[:, C_RSTS:C_RSTS + 128] = (ar % 32 != 0).astype(np.float32)[None, :]
    c[:, C_ONES:C_ONES + 128] = 1.0
    c[:, C_AMASK:C_AMASK + 128] = (s_ // 64 <= t_ // 64).astype(np.float32)
    c[:, C_SMASK:C_SMASK + 128] = (s_ // 32 == t_ // 32).astype(np.float32)
    c[:, C_GTRILP:C_GTRILP + 128] = (t_ <= s_).astype(np.float32)
    c[:, C_GTRILS:C_GTRILS + 128] = ((t_ <= s_) & (s_ // 32 == t_ // 32)).astype(np.float32)
    half = 16
    inv = (10000.0 ** (-np.arange(half, dtype=np.float32) / half)).astype(np.float32)
    pos = np.arange(SEQ, dtype=np.float32)
    ang = (pos[:, None] * inv[None, :]).astype(np.float32)
    cosp = np.cos(ang).astype(np.float32).reshape(16, 128, 16).transpose(1, 0, 2).reshape(128, 256)
    sinp = np.sin(ang).astype(np.float32).reshape(16, 128, 16).transpose(1, 0, 2).reshape(128, 256)
    c[:, C_ROPEP:C_ROPEP + 256] = cosp
    c[:, C_ROPEP + 256:C_ROPEP + 512] = sinp
    poss = (PAST + (ar % 32)).astype(np.float32)
    angs = (poss[:, None] * inv[None, :]).astype(np.float32)
    c[:, C_ROPES:C_ROPES + 16] = np.cos(angs)
    c[:, C_ROPES + 16:C_ROPES + 32] = np.sin(angs)
    return c


def host_params(inp):
    p = np.zeros((128, NPARAM), np.float32)
    for l in range(2):
        for wi, nm in enumerate(["ffn1_norm", "mix_norm", "ffn2_norm"]):
            col = P_GAIN + (3 * l + wi) * 8
            p[:, col:col + 8] = inp[nm][l].reshape(8, 128).T
    p[:, P_LRE:P_LRE + 16] = inp["s5_lambda_re"][0].reshape(16, 128).T
    p[:, P_LIM:P_LIM + 16] = inp["s5_lambda_im"][0].reshape(16, 128).T
    p[:, P_LOGDT:P_LOGDT + 16] = np.repeat(inp["s5_log_dt"][0], 64).reshape(16, 128).T
    p[:, P_SD:P_SD + 4] = inp["s5_d"][0].reshape(4, 128).T
    p[:, P_BGLU:P_BGLU + 4] = inp["s5_b_glu"][0].reshape(4, 128).T
    p[:, P_LB:P_LB + 12] = inp["hgrn_lb_logits"].reshape(3, 4, 128).transpose(2, 0, 1).reshape(128, 12)
    p[:, P_OUTG] = inp["hgrn_out_norm"][0]
    p[:, P_VNORM:P_VNORM + 512] = inp["gmlp_v_norm"][0][None, :]
    p[:, P_QNORM:P_QNORM + 256] = inp["mla_q_norm"][0][None, :]
    p[:, P_KVNORM:P_KVNORM + 128] = inp["mla_kv_norm"][0][None, :]
    p[:, P_QGAIN:P_QGAIN + 96] = inp["mla_q_gain"][0][None, :]
    p[:, P_KGAIN:P_KGAIN + 96] = inp["mla_k_gain"][0][None, :]
    bs = inp["gmlp_b_s"][0]
    p[:, P_BS:P_BS + 512] = bs.reshape(1, 512)
    p[:, P_BSS:P_BSS + 512] = np.tile(bs[:, :32], (1, 4)).reshape(1, 512)
    return p


def host_wsmall(inp):
    w = np.zeros((128, NSMALL), np.float32)
    w[:, W_GLU:W_GLU + 2048] = inp["s5_w_glu"][0].reshape(4, 128, 4, 128).transpose(1, 0, 2, 3).reshape(128, 2048)
    w[:, W_UQ:W_UQ + 1536] = inp["mla_w_uq"][0].reshape(2, 128, 768).transpose(1, 0, 2).reshape(128, 1536)
    w[:, W_UKV:W_UKV + 1024] = inp["mla_w_ukv"][0]
    return w


def host_cmats(inp):
    out = np.zeros((128, 2, 16, 128), np.float32)
    for pi, nm in enumerate(["s5_c_re", "s5_c_im"]):
        cm = inp[nm][0]
        for g in range(32):
            i = g // 2
            st = (g % 2) * 64
            cl = (g % 8) * 16
            out[st:st + 64, pi, i, cl:cl + 16] = cm[g].T
    return out.reshape(128, 4096)


def host_wstream(inp):
    ws = np.zeros((NUNITS, 128, 4096), np.float32)
    for l in range(2):
        for wi, pre in enumerate(["ffn1", "ffn2"]):
            wg = inp[pre + "_w_gate"][l]
            wu = inp[pre + "_w_up"][l]
            wdn = inp[pre + "_w_down"][l]
            g4 = wg.reshape(8, 128, NJ, 128).transpose(2, 1, 0, 3)
            u4 = wu.reshape(8, 128, NJ, 128).transpose(2, 1, 0, 3)
            base = U_FFN + (2 * l + wi) * 17
            for u in range(11):
                blk = np.stack([g4[2 * u], u4[2 * u], g4[2 * u + 1], u4[2 * u + 1]], axis=1)
                ws[base + u] = blk.reshape(128, 4096)
            d4 = wdn.reshape(NJ, 128, 2, 512)
            for hf in range(2):
                for ud in range(3):
                    nj = 8 if ud < 2 else 6
                    blk = np.zeros((128, 8, 512), np.float32)
                    blk[:, :nj, :] = d4[8 * ud:8 * ud + nj, :, hf, :].transpose(1, 0, 2)
                    ws[base + 11 + 3 * hf + ud] = blk.reshape(128, 4096)
    wab = inp["ab_w_in"][0]
    for k, col0 in enumerate([0, 512, 1536, 2048]):
        ws[U_ABIN + k] = fm_unit(wab, col0).reshape(128, 4096)
    ws[U_ABIN + 4] = tm_unit(wab, 1024, 512).reshape(128, 4096)
    for hf in range(2):
        ws[U_ABOUT + hf] = tm_unit(inp["ab_w_out"][0], hf * 512, 512).reshape(128, 4096)
        ws[U_CDOUT + hf] = tm_unit(inp["cd_w_out"][0], hf * 512, 512).reshape(128, 4096)
    for nm, col in (("s5_b_re", 0), ("s5_b_im", 2048)):
        bb = inp[nm][0]
        bt = np.zeros((128, 16, 128), np.float32)
        for g in range(32):
            i = g // 2
            cl = (g % 8) * 16
            st = (g % 2) * 64
            bt[cl:cl + 16, i, st:st + 64] = bb[g].T
        ws[U_BT, :, col:col + 2048] = bt.reshape(128, 2048)
    wcd = inp["cd_w_in"][0]
    ws[U_CDIN + 0] = fm_unit(wcd, 0).reshape(128, 4096)
    ws[U_CDIN + 1] = tm_unit(wcd, 512, 512).reshape(128, 4096)
    ws[U_CDIN + 2] = tm_unit(wcd, 1024, 416).reshape(128, 4096)
    return ws


_PROG = {}


def _get_prog(stop="full", debug=False):
    key = (stop, debug)
    if key not in _PROG:
        _PROG[key] = build_program(stop, debug)
    return _PROG[key]


def kernel(_stop="full", _debug=False, **inp):
    inp = {k: np.asarray(v, dtype=np.float32) for k, v in inp.items()}
    nc = _get_prog(_stop, _debug)
    consts = host_consts()
    params = host_params(inp)
    wstream = host_wstream(inp)
    wsmall = host_wsmall(inp)
    cmats = host_cmats(inp)
    gws = np.ascontiguousarray(inp["gmlp_w_s"][0].transpose(1, 0, 2).reshape(128, 512))
    in_maps = []
    for c in range(N_CORES):
        sb_ = slice(4 * c, 4 * c + 4)
        x0 = np.stack([inp["state_s5_re"][0, sb_].reshape(4, 16, 128),
                       inp["state_s5_im"][0, sb_].reshape(4, 16, 128)], axis=0)
        m = {
            "xp": np.ascontiguousarray(inp["x_prompt"][2 * c:2 * c + 2]),
            "xs": np.ascontiguousarray(inp["x_sample"][sb_].reshape(128, D)),
            "wstream": wstream, "consts": consts, "params": params, "wsmall": wsmall, "cmats": cmats, "gws": gws,
            "s5x0": np.ascontiguousarray(x0.transpose(3, 0, 2, 1).reshape(128, 128)),
            "hs0": np.ascontiguousarray(inp["state_hgrn"][0, sb_]),
            "pckv": np.ascontiguousarray(inp["cache_mla_ckv"][0, sb_]),
            "pkpe": np.ascontiguousarray(inp["cache_mla_kpe"][0, sb_]),
        }
        in_maps.append(m)
    res = run_bass_kernel_spmd(nc, in_maps, core_ids=list(range(N_CORES)))
    r = res.results

    def cat(name, shape):
        return np.concatenate([r[c][name].reshape(shape) for c in range(N_CORES)], axis=0)

    y_prompt = cat("yp", (2, SEQ, D))
    y_sample = cat("ys", (4, 32, D))
    hg_p = cat("o_hgp", (2, 4, 128, 128))[None]
    hg_s = cat("o_hgs", (4, 4, 128, 128))[None]
    s5p = cat("o_s5p", (2, 2, 32, 64))
    s5s = cat("o_s5s", (4, 2, 32, 64))
    gv_p = cat("o_gvp", (2, 128, 512))[None]
    gv_s = cat("o_gvs", (4, 32, 512))[None]
    ckv_p = cat("o_ckvp", (2, SEQ, 128))[None]
    kpe_p = cat("o_kpep", (2, SEQ, 32))[None]
    ckv_s = cat("o_ckvs", (4, 32, 128))[None]
    kpe_s = cat("o_kpes", (4, 32, 32))[None]
    return (y_prompt, y_sample, hg_p, hg_s,
            np.ascontiguousarray(s5p[:, 0])[None], np.ascontiguousarray(s5p[:, 1])[None],
            np.ascontiguousarray(s5s[:, 0])[None], np.ascontiguousarray(s5s[:, 1])[None],
            gv_p, gv_s, ckv_p, kpe_p, ckv_s, kpe_s)
```

```python
from contextlib import ExitStack
import numpy as np
import concourse.bass as bass
import concourse.mybir as mybir
from concourse.bass_utils import run_bass_kernel_spmd

F32 = mybir.dt.float32
BF16 = mybir.dt.bfloat16
AF = mybir.ActivationFunctionType
ALU = mybir.AluOpType
AX = mybir.AxisListType

N_CORES = 8
D = 1024
DFF = 2816
NJ = DFF // 128
SEQ = 2048
EPS = 1e-6
PAST = 1024

EPOCH = 30000
NDMA = 8


class Tok:
    __slots__ = ("w", "r")

    def __init__(self):
        self.w = None
        self.r = {}


def toks(n):
    return [Tok() for _ in range(n)]


class Sched:
    ENGS = ["pe", "act", "dve", "pool", "sp"]

    def __init__(self, nc):
        self.nc = nc
        self.q = {e: [] for e in self.ENGS}
        self.cnt = {}
        self.seen = {e: {} for e in self.ENGS}
        self.icount = {e: 0 for e in self.ENGS}
        self.dma_rr = {e: 0 for e in self.ENGS}
        self.keys = []

    def _key_count(self, key):
        if key not in self.cnt:
            self.cnt[key] = 0
            self.keys.append(key)
        return self.cnt[key]

    def _need(self, eng, waits, key, val, same_ok):
        if val <= 0:
            return
        if key[0] == eng and (eng == "pe" or same_ok):
            return
        if self.seen[eng].get(key, 0) >= val:
            return
        if waits.get(key, 0) < val:
            waits[key] = val

    def op(self, eng, fn, reads=(), writes=(), dma=False):
        waits = {}
        for t in reads:
            if t.w is not None:
                self._need(eng, waits, t.w[0], t.w[1], False)
        for t in writes:
            if t.w is not None:
                self._need(eng, waits, t.w[0], t.w[1], True)
            for k, v in t.r.items():
                self._need(eng, waits, k, v, True)
        if dma:
            slot = self.dma_rr[eng] % NDMA
            self.dma_rr[eng] += 1
            key = ("dma_" + eng, slot)
            prev = self._key_count(key)
            if prev > 0 and self.seen[eng].get(key, 0) < prev:
                waits[key] = max(waits.get(key, 0), prev)
            inc = 16
        else:
            ep = self.icount[eng] // EPOCH
            self.icount[eng] += 1
            key = (eng, ep)
            self._key_count(key)
            inc = 1
        for k, v in waits.items():
            self.seen[eng][k] = v
        c = self.cnt[key] + inc
        self.cnt[key] = c
        for t in reads:
            if t.r.get(key, 0) < c:
                t.r[key] = c
        for t in writes:
            t.w = (key, c)
            t.r = {}
        self.q[eng].append((waits, fn, key, inc))

    def finish(self):
        for eng in self.ENGS:
            waits = {}
            for key in self.keys:
                if key[0] == "dma_" + eng and self.cnt[key] > 0:
                    waits[key] = self.cnt[key]
            if waits:
                self.q[eng].append((waits, None, None, 0))

    def build(self):
        nc = self.nc
        with ExitStack() as es:
            sems = {}
            for key in self.keys:
                sems[key] = es.enter_context(nc.semaphore("s_%s_%s" % (key[0], key[1])))
            block = es.enter_context(nc.Block())

            def replay(eng_name):
                def run(e):
                    for waits, fn, key, inc in self.q[eng_name]:
                        for k, v in waits.items():
                            e.wait_ge(sems[k], v)
                        if fn is not None:
                            fn(e).then_inc(sems[key], inc)
                return run

            if self.q["pe"]:
                block.tensor(replay("pe"))
            if self.q["act"]:
                block.scalar(replay("act"))
            if self.q["dve"]:
                block.vector(replay("dve"))
            if self.q["pool"]:
                block.gpsimd(replay("pool"))
            if self.q["sp"]:
                block.sync(replay("sp"))


def alias_barrier(dst, src):
    merged = {}
    for t in src:
        if t.w is not None:
            k, v = t.w
            if merged.get(k, 0) < v:
                merged[k] = v
        for k, v in t.r.items():
            if merged.get(k, 0) < v:
                merged[k] = v
    for t in dst:
        for k, v in merged.items():
            if t.r.get(k, 0) < v:
                t.r[k] = v


def fm_unit(w, col0, nchunks=4):
    k = w.shape[0] // 128
    blk = w[:, col0:col0 + 128 * nchunks].reshape(k, 128, nchunks, 128)
    return np.ascontiguousarray(blk.transpose(1, 2, 0, 3))


def tm_unit(w, col0, ncols):
    k = w.shape[0] // 128
    out = np.zeros((128, k, 512), np.float32)
    out[:, :, :ncols] = w[:, col0:col0 + ncols].reshape(k, 128, ncols).transpose(1, 0, 2)
    return out


def build_program(stop="full", debug=False):
    nc = bass.Bass("TRN2", target_bir_lowering=False)
    S = Sched(nc)
    es = ExitStack()

    def dram_in(name, shape):
        return nc.dram_tensor(name, list(shape), F32, kind="ExternalInput").ap()

    def dram_out(name, shape):
        return nc.dram_tensor(name, list(shape), F32, kind="ExternalOutput").ap()

    def sb(name, shape, dt=F32):
        return es.enter_context(nc.sbuf_tensor(name, list(shape), dt))

    xp = dram_in("xp", [2, SEQ, D])
    xs = dram_in("xs", [128, D])
    wstream = dram_in("wstream", [NUNITS, 128, 4096])
    consts = dram_in("consts", [128, NCONST])
    params = dram_in("params", [128, NPARAM])
    wsmall_d = dram_in("wsmall", [128, NSMALL])
    cmats_d = dram_in("cmats", [128, 4096])
    gws_d = dram_in("gws", [128, 512])
    s5x0_d = dram_in("s5x0", [128, 128])
    hs0_d = dram_in("hs0", [4, 4, 128, 128])
    pckv_d = dram_in("pckv", [4, PAST, 128])
    pkpe_d = dram_in("pkpe", [4, PAST, 32])

    yp = dram_out("yp", [2, SEQ, D])
    ys = dram_out("ys", [128, D])
    o_hgp = dram_out("o_hgp", [2, 4, 128, 128])
    o_hgs = dram_out("o_hgs", [4, 4, 128, 128])
    o_s5p = dram_out("o_s5p", [2, 2, 16, 128])
    o_s5s = dram_out("o_s5s", [4, 2, 16, 128])
    o_gvp = dram_out("o_gvp", [2, 128, 512])
    o_gvs = dram_out("o_gvs", [128, 512])
    o_ckvp = dram_out("o_ckvp", [2, SEQ, 128])
    o_kpep = dram_out("o_kpep", [2, SEQ, 32])
    o_ckvs = dram_out("o_ckvs", [128, 128])
    o_kpes = dram_out("o_kpes", [128, 32])

    x = sb("x", [128, 4, D]); x_t = toks(4)
    hT = sb("hT", [128, 8, 512], BF16); hT_t = toks(4)
    ring = [sb("ring%d" % i, [128, 4096], BF16) for i in range(NRING)]
    ring_t = toks(NRING)
    ring_rr = [0]
    ARENA = 21760
    arena = sb("arena", [128, ARENA], BF16)
    arena_off = [0]
    scratch_toks = []

    def carve(shape, dt=F32, ntok=1):
        n = 1
        for s_ in shape[1:]:
            n *= s_
        nb16 = n * (2 if dt == F32 else 1)
        nb16 = (nb16 + 1) // 2 * 2
        off = arena_off[0]
        arena_off[0] += nb16
        assert arena_off[0] <= ARENA, ("arena overflow", arena_off[0])
        v = arena[:, off:off + nb16]
        if dt == F32:
            v = v.bitcast(F32)
        v = v[:, 0:n]
        if len(shape) == 3:
            v = v.rearrange("p (a b) -> p a b", a=shape[1])
        elif len(shape) == 4:
            v = v.rearrange("p (a b c) -> p a b c", a=shape[1], b=shape[2])
        tk = toks(ntok)
        scratch_toks.extend(tk)
        return (v, tk[0]) if ntok == 1 else (v, tk)

    actb = arena[:, 0:NJ * 512].rearrange("p (j n) -> p j n", j=NJ); act_t = toks(NJ)
    sg = arena[:, NJ * 512:NJ * 512 + 2048].bitcast(F32).rearrange("p (s n) -> p s n", s=2); sg_t = toks(2)
    ffn_toks = act_t + sg_t
    cst = sb("cst", [128, NCONST]); cst_t = Tok()
    prm = sb("prm", [128, NPARAM]); prm_t = Tok()
    wsm = sb("wsm", [128, NSMALL], BF16); wsm_t = Tok()
    ident = sb("ident", [128, 128], BF16); ident_t = Tok()
    xn = sb("xn", [128, 2, D], BF16); xn_t = toks(2)
    stat = sb("stat", [128, 8, 4]); stat_t = toks(8)
    stat_rr = [0]

    ps = [es.enter_context(nc.psum_tensor("ps%d" % b, [128, 512], F32)) for b in range(8)]
    ps_t = toks(8)
    ps_rr = [0]

    ps_lo = [0]

    def pbank(lo=False):
        if lo:
            b = ps_lo[0] % 4
            ps_lo[0] += 1
            return b
        b = ps_rr[0] % 8
        ps_rr[0] += 1
        return b

    def dma(eng, out, in_, reads=(), writes=()):
        S.op(eng, lambda e: e.dma_start(out=out, in_=in_), reads=reads, writes=writes, dma=True)

    def mm(out, lhsT, rhs, start, stop, reads, writes, tp=None):
        if tp is None:
            S.op("pe", lambda e: e.matmul(out, lhsT=lhsT, rhs=rhs, start=start, stop=stop), reads=reads, writes=writes)
        else:
            S.op("pe", lambda e: e.matmul(out, lhsT=lhsT, rhs=rhs, start=start, stop=stop, tile_position=tp),
                 reads=reads, writes=writes)

    def tr(out, in_, reads, writes, idn=None):
        idap = ident[:] if idn is None else idn
        S.op("pe", lambda e: e.transpose(out, in_, idap), reads=list(reads) + [ident_t, cst_t], writes=writes)

    def act(out, in_, func, reads, writes, scale=1.0, bias=0.0, accum=None):
        if accum is None:
            S.op("act", lambda e: e.activation(out=out, in_=in_, func=func, bias=bias, scale=scale),
                 reads=reads, writes=writes)
        else:
            S.op("act", lambda e: e.activation(out=out, in_=in_, func=func, bias=bias, scale=scale, accum_out=accum),
                 reads=reads, writes=writes)

    def tt(out, in0, in1, op, reads, writes, eng="dve"):
        S.op(eng, lambda e: e.tensor_tensor(out=out, in0=in0, in1=in1, op=op), reads=reads, writes=writes)

    def ts(out, in0, s1, s2, op0, op1, reads, writes, eng="dve"):
        if s2 is None:
            S.op(eng, lambda e: e.tensor_scalar(out=out, in0=in0, scalar1=s1, scalar2=None, op0=op0),
                 reads=reads, writes=writes)
        else:
            S.op(eng, lambda e: e.tensor_scalar(out=out, in0=in0, scalar1=s1, scalar2=s2, op0=op0, op1=op1),
                 reads=reads, writes=writes)

    def stt(out, in0, scalar, in1, op0, op1, reads, writes):
        S.op("dve", lambda e: e.scalar_tensor_tensor(out=out, in0=in0, scalar=scalar, in1=in1, op0=op0, op1=op1),
             reads=reads, writes=writes)

    def cp(out, in_, reads, writes, eng="dve"):
        S.op(eng, lambda e: e.tensor_copy(out=out, in_=in_), reads=reads, writes=writes)

    def recip(out, in_, reads, writes):
        S.op("dve", lambda e: e.reciprocal(out=out, in_=in_), reads=reads, writes=writes)

    def scan(out, d0, d1, init, reads, writes):
        S.op("dve", lambda e: e.tensor_tensor_scan(out=out, data0=d0, data1=d1, initial=init, op0=ALU.mult, op1=ALU.add),
             reads=reads, writes=writes)

    def memset(ap, val, writes, eng="dve"):
        S.op(eng, lambda e: e.memset(ap, val), writes=writes)

    def reduce_sum(out, in_, reads, writes):
        S.op("dve", lambda e: e.tensor_reduce(out=out, in_=in_, axis=AX.X, op=ALU.add), reads=reads, writes=writes)

    def load_unit(uidx):
        sl = ring_rr[0] % NRING
        ring_rr[0] += 1
        dma("pool", ring[sl][:], wstream[uidx], writes=[ring_t[sl]])
        return ring[sl], ring_t[sl]

    def new_stat():
        i = stat_rr[0] % 8
        stat_rr[0] += 1
        return stat[:, i, :], stat_t[i]

    class _Stop(Exception):
        pass

    def ck(n):
        if stop == "ck%d" % n:
            raise _Stop()

    def bc(ap2, shape, axis):
        return ap2.unsqueeze(axis).to_broadcast(shape)

    dma("sp", cst[:], consts, writes=[cst_t])
    dma("sp", prm[:], params, writes=[prm_t])
    dma("pool", wsm[:], wsmall_d, writes=[wsm_t])
    cp(ident[:], cst[:, C_IDENT:C_IDENT + 128], [cst_t], [ident_t])
    ident32 = cst[:, C_IDENT:C_IDENT + 128]

    def rmsnorm_to_hT(NT, gcol):
        for a in range(NT):
            st, st_t = new_stat()
            b = a % 2
            act(xn[:, b, :], x[:, a, :], AF.Square, [x_t[a]], [xn_t[b], st_t], accum=st[:, 0:1])
            act(st[:, 1:2], st[:, 0:1], AF.Sqrt, [st_t], [st_t], scale=1.0 / D, bias=EPS)
            recip(st[:, 2:3], st[:, 1:2], [st_t], [st_t])
            act(xn[:, b, :], x[:, a, :], AF.Copy, [x_t[a], st_t], [xn_t[b]], scale=st[:, 2:3])
            pv = ps[b][:].bitcast(BF16)
            for c in range(8):
                tr(pv[:, c * 128:(c + 1) * 128], xn[:, b, c * 128:(c + 1) * 128], [xn_t[b]], [ps_t[b]])
            tt(hT[:, :, a * 128:(a + 1) * 128], pv.rearrange("p (c t) -> p c t", c=8),
               bc(prm[:, gcol:gcol + 8], [128, 8, 128], 2), ALU.mult, [ps_t[b], prm_t], [hT_t[a]])

    def ffn(NT, ubase):
        T = NT * 128
        for u in range(NJ // 2):
            rb, rt = load_unit(ubase + u)
            rv = rb[:].rearrange("p (q k f) -> p q k f", q=4, k=8)
            for jj in range(2):
                j = 2 * u + jj
                bg, bu = (j % 2), 2 + (j % 2)
                for k in range(8):
                    mm(ps[bg][:, 0:T], rv[:, 2 * jj, k, :], hT[:, k, 0:T], k == 0, k == 7,
                       [rt] + hT_t[:NT], [ps_t[bg]])
                for k in range(8):
                    mm(ps[bu][:, 0:T], rv[:, 2 * jj + 1, k, :], hT[:, k, 0:T], k == 0, k == 7,
                       [rt] + hT_t[:NT], [ps_t[bu]])
                s = j % 2
                act(sg[:, s, 0:T], ps[bg][:, 0:T], AF.Silu, [ps_t[bg]], [sg_t[s]])
                tt(actb[:, j, 0:T], sg[:, s, 0:T], ps[bu][:, 0:T], ALU.mult, [sg_t[s], ps_t[bu]], [act_t[j]])
        for hf in range(2):
            bb = 4 if hf == 0 else 0
            for ud in range(3):
                rb, rt = load_unit(ubase + 11 + 3 * hf + ud)
                rv = rb[:].rearrange("p (j n) -> p j n", j=8)
                for jj in range(8 if ud < 2 else 6):
                    j = 8 * ud + jj
                    for a in range(NT):
                        mm(ps[bb + a][:, :], actb[:, j, a * 128:(a + 1) * 128], rv[:, jj, :],
                           j == 0, j == NJ - 1, [act_t[j], rt], [ps_t[bb + a]])
            for a in range(NT):
                stt(x[:, a, hf * 512:(hf + 1) * 512], ps[bb + a][:, :], 0.5, x[:, a, hf * 512:(hf + 1) * 512],
                    ALU.mult, ALU.add, [ps_t[bb + a], x_t[a]], [x_t[a]])

    def out_proj(NT, a0, ubase, aT, aT_t):
        for hf in range(2):
            rb, rt = load_unit(ubase + hf)
            rv = rb[:].rearrange("p (k n) -> p k n", k=8)
            for a in range(NT):
                b = pbank()
                for k in range(8):
                    mm(ps[b][:, :], aT[:, k, a * 128:(a + 1) * 128], rv[:, k, :], k == 0, k == 7,
                       [rt] + list(aT_t), [ps_t[b]])
                tt(x[:, a0 + a, hf * 512:(hf + 1) * 512], ps[b][:, :], x[:, a0 + a, hf * 512:(hf + 1) * 512], ALU.add,
                   [ps_t[b], x_t[a0 + a]], [x_t[a0 + a]])

    pvv = sb("pvv", [128, 256]); pvv_t = Tok()
    V_LB, V_OML, V_R, V_CA, V_CB, V_RN2, V_TH = 0, 4, 8, 24, 40, 56, 72
    V_TMP = 88
    cosT = sb("cosT", [128, 16, 128]); sinT = sb("sinT", [128, 16, 128]); tab_t = Tok()
    cre = sb("cre", [128, 16, 128], BF16); ncim = sb("ncim", [128, 16, 128], BF16); cm_t = Tok()
    ones128 = sb("ones128", [128, 128], BF16); ones_t = Tok()
    maskP = sb("maskP", [128, 128], BF16); maskS = sb("maskS", [128, 128], BF16); mask_t = Tok()
    Sst = sb("Sst", [128, 4, 128]); Sst_t = toks(4)
    Sbf = sb("Sbf", [128, 4, 128], BF16); Sbf_t = toks(4)
    car = sb("car", [128, 2, 16, 4]); car_t = Tok()
    x0p = sb("x0p", [128, 2, 16, 4]); x0p_t = Tok()
    MT = 256
    abT, abT_t = carve([128, 8, MT], BF16, 8)
    qs, qk_t = carve([128, 4, MT], BF16, 4)
    ks, _ks_t = carve([128, 4, MT], BF16)
    gateT, gate_t = carve([128, 4, MT], BF16, 4)
    ivb, iv_t = carve([128, 2, 512], BF16, 2)
    uT, uT_t = carve([128, 4, MT], F32, 4)
    ht, ht_t = [], []
    for _i in range(4):
        _v, _t = carve([128, MT]); ht.append(_v); ht_t.append(_t)
    lt, lt_t = [], []
    for _i in range(6):
        _v, _t = carve([128, 512]); lt.append(_v); lt_t.append(_t)
    lt_rr = [0]
    xrb, xb_t = carve([128, 512], BF16)
    xib, _xib_t = carve([128, 512], BF16)
    ub, ub_t = carve([128, 128], BF16)
    ub2, ub2_t = carve([128, 128], BF16)
    ubs, ubs_t = [ub, ub2], [ub_t, ub2_t]
    sm, sm_t = [], []
    for _i in range(6):
        _v, _t = carve([128, 128]); sm.append(_v); sm_t.append(_t)
    sm_rr = [0]
    smb, smb_t = [], []
    for _i in range(4):
        _v, _t = carve([128, 128], BF16); smb.append(_v); smb_t.append(_t)
    smb_rr = [0]
    khm, khm_t = carve([128, 4, 128], BF16)
    zf, z_t = carve([128, 4, 128], F32, 4)
    zb, _zb_t = carve([128, 4, 128], BF16)
    L0_toks = list(scratch_toks)
    L1_toks_ref = []
    L0_end = arena_off[0]

    def nsm():
        i = sm_rr[0] % 6
        sm_rr[0] += 1
        return sm[i], sm_t[i]

    def nsmb():
        i = smb_rr[0] % 4
        smb_rr[0] += 1
        return smb[i], smb_t[i]

    def nlt():
        i = lt_rr[0] % 6
        lt_rr[0] += 1
        return lt[i], lt_t[i]

    def prep_layer0():
        P = prm
        V = pvv
        rd = [prm_t, pvv_t, cst_t]
        wr = [pvv_t]

        def col(c0, n=16):
            return V[:, c0:c0 + n]
        T0 = [col(V_TMP + 16 * k) for k in range(10)]
        e3 = V[:, 248:256]
        act(V[:, 232:244], P[:, P_LB:P_LB + 12], AF.Exp, rd, wr)
        tt(T0[0][:, 0:4], V[:, 232:236], V[:, 236:240], ALU.add, rd, wr)
        tt(T0[0][:, 0:4], T0[0][:, 0:4], V[:, 240:244], ALU.add, rd, wr)
        recip(T0[0][:, 4:8], T0[0][:, 0:4], rd, wr)
        tt(col(V_LB, 4), V[:, 232:236], T0[0][:, 4:8], ALU.mult, rd, wr)
        ts(col(V_OML, 4), col(V_LB, 4), -1.0, 1.0, ALU.mult, ALU.add, rd, wr)
        dtc = T0[1]
        act(dtc, P[:, P_LOGDT:P_LOGDT + 16], AF.Exp, rd, wr)
        lre = T0[2]
        ts(lre, P[:, P_LRE:P_LRE + 16], -1e-4, None, ALU.min, None, rd, wr)
        tt(T0[3], lre, dtc, ALU.mult, rd, wr)
        act(col(V_R), T0[3], AF.Exp, rd, wr)
        tt(col(V_TH), P[:, P_LIM:P_LIM + 16], dtc, ALU.mult, rd, wr)
        for q4 in range(4):
            isl = slice(4 * q4, 4 * q4 + 4)
            A, At = lt[0], lt_t[0]
            Bt_, Btt = lt[1], lt_t[1]
            Ci, Cit = lt[2], lt_t[2]
            Av = A.rearrange("p (i j) -> p i j", i=4)
            tt(Av, bc(V[:, V_TH + 4 * q4:V_TH + 4 * q4 + 4], [128, 4, 128], 2),
               bc(cst[:, C_JV:C_JV + 128], [128, 4, 128], 1), ALU.mult, rd, [At])
            for which, tab in ((0, sinT), (1, cosT)):
                off = 64.0 + (0.25 if which == 1 else 0.0)
                ts(Bt_, A, 1.0 / (2.0 * np.pi), off, ALU.mult, ALU.add, [At], [Btt])
                S.op("dve", lambda e, o=Ci.bitcast(mybir.dt.int32), i_=Bt_: e.tensor_copy(out=o, in_=i_),
                     reads=[Btt], writes=[Cit])
                D2, D2t = lt[3], lt_t[3]
                S.op("dve", lambda e, o=D2, i_=Ci.bitcast(mybir.dt.int32): e.tensor_copy(out=o, in_=i_),
                     reads=[Cit], writes=[D2t])
                tt(Bt_, Bt_, D2, ALU.subtract, [Btt, D2t], [Btt])
                ts(D2, Bt_, 0.5, None, ALU.is_ge, None, [Btt], [D2t])
                tt(Bt_, Bt_, D2, ALU.subtract, [Btt, D2t], [Btt])
                act(tab[:, isl, :], Bt_.rearrange("p (i j) -> p i j", i=4), AF.Sin, [Btt], [tab_t],
                    scale=2.0 * np.pi)
        rd2 = rd + [tab_t]
        ur, ui_, tr_, ti_, pr, pi_, t1_, t2_ = T0[3], T0[0], T0[4], T0[5], T0[6], T0[7], T0[8], T0[9]
        ts(ur, T0[3], 1.0 / 256.0, None, ALU.mult, None, rd, wr)
        ts(ui_, col(V_TH), 1.0 / 256.0, None, ALU.mult, None, rd, wr)
        ts(tr_, ur, 1.0 / 6.0, 1.0, ALU.mult, ALU.add, rd, wr)
        ts(ti_, ui_, 1.0 / 6.0, None, ALU.mult, None, rd, wr)
        for kk in (5.0, 4.0, 3.0, 2.0, None):
            tt(t1_, ur, tr_, ALU.mult, rd, wr)
            tt(t2_, ui_, ti_, ALU.mult, rd, wr)
            tt(pr, t1_, t2_, ALU.subtract, rd, wr)
            tt(t1_, ur, ti_, ALU.mult, rd, wr)
            tt(t2_, ui_, tr_, ALU.mult, rd, wr)
            tt(pi_, t1_, t2_, ALU.add, rd, wr)
            if kk is not None:
                ts(tr_, pr, 1.0 / kk, 1.0, ALU.mult, ALU.add, rd, wr)
                ts(ti_, pi_, 1.0 / kk, None, ALU.mult, None, rd, wr)
        for _sq in range(8):
            tt(t1_, pr, pr, ALU.mult, rd, wr)
            tt(t2_, pi_, pi_, ALU.mult, rd, wr)
            tt(t1_, t1_, t2_, ALU.subtract, rd, wr)
            stt(t2_, pr, 1.0, pi_, ALU.add, ALU.mult, rd, wr)
            stt(pr, pr, 2.0, t1_, ALU.mult, ALU.add, rd, wr)
            ts(pi_, t2_, 2.0, None, ALU.mult, None, rd, wr)
        nr, ni, den = pr, pi_, T0[4]
        T0 = list(T0)
        T0[7], T0[8] = T0[5], T0[9]
        lim = P[:, P_LIM:P_LIM + 16]
        tt(den, lre, lre, ALU.mult, rd2, wr)
        tt(T0[7], lim, lim, ALU.mult, rd2, wr)
        tt(den, den, T0[7], ALU.add, rd2, wr)
        recip(den, den, rd2, wr)
        tt(T0[7], nr, lre, ALU.mult, rd2, wr)
        tt(T0[8], ni, lim, ALU.mult, rd2, wr)
        tt(T0[7], T0[7], T0[8], ALU.add, rd2, wr)
        tt(col(V_CA), T0[7], den, ALU.mult, rd2, wr)
        tt(T0[7], ni, lre, ALU.mult, rd2, wr)
        tt(T0[8], nr, lim, ALU.mult, rd2, wr)
        tt(T0[7], T0[7], T0[8], ALU.subtract, rd2, wr)
        tt(col(V_CB), T0[7], den, ALU.mult, rd2, wr)
        tt(T0[7], col(V_CA), col(V_CA), ALU.mult, rd2, wr)
        tt(T0[8], col(V_CB), col(V_CB), ALU.mult, rd2, wr)
        tt(T0[7], T0[7], T0[8], ALU.add, rd2, wr)
        recip(col(V_RN2), T0[7], rd2, wr)
        scr = arena[:, 0:8192].bitcast(F32)
        scr_t = Tok()
        cmr = scr[:, 0:2048].rearrange("p (i c) -> p i c", i=16)
        cmi = scr[:, 2048:4096].rearrange("p (i c) -> p i c", i=16)
        dma("sp", scr[:, 0:4096], cmats_d, writes=[scr_t])
        ca_b = bc(col(V_CA), [128, 16, 128], 2)
        cb_b = bc(col(V_CB), [128, 16, 128], 2)
        t1 = lt[4].rearrange("p (i c) -> p i c", i=4)
        t2 = lt[5].rearrange("p (i c) -> p i c", i=4)
        for q4 in range(4):
            isl = slice(4 * q4, 4 * q4 + 4)
            cab = bc(V[:, V_CA + 4 * q4:V_CA + 4 * q4 + 4], [128, 4, 128], 2)
            cbb = bc(V[:, V_CB + 4 * q4:V_CB + 4 * q4 + 4], [128, 4, 128], 2)
            tt(t1, cmr[:, isl, :], cab, ALU.mult, [scr_t, pvv_t], [lt_t[4]])
            tt(t2, cmi[:, isl, :], cbb, ALU.mult, [scr_t, pvv_t], [lt_t[5]])
            tt(cre[:, isl, :], t1, t2, ALU.subtract, [lt_t[4], lt_t[5]], [cm_t])
            tt(t1, cmr[:, isl, :], cbb, ALU.mult, [scr_t, pvv_t], [lt_t[4]])
            tt(t2, cmi[:, isl, :], cab, ALU.mult, [scr_t, pvv_t], [lt_t[5]])
            stt(ncim[:, isl, :], t1, -1.0, t2, ALU.mult, ALU.subtract, [lt_t[4], lt_t[5]], [cm_t])
        alias_barrier(ffn_toks, [scr_t])
        alias_barrier(L0_toks, [scr_t])
        ts(ones128[:], cst[:, C_ONES:C_ONES + 128], 1.0 / 128.0, None, ALU.mult, None, [cst_t], [ones_t])
        cp(maskP[:], cst[:, C_MASKP:C_MASKP + 128], [cst_t], [mask_t])
        cp(maskS[:], cst[:, C_MASKS:C_MASKS + 128], [cst_t], [mask_t])

    prep_layer0()

    def mixer0(NT, kind, seq, g):
        alias_barrier(L0_toks, ffn_toks + L1_toks_ref)
        nh = (NT + 1) // 2
        for half in range(nh):
            a0 = 2 * half
            nt = min(2, NT - a0)
            mixer0_half(nt, a0, kind, seq, g, first=(kind == "p" and g == 0 and half == 0),
                        last=(kind == "s") or (kind == "p" and g == 3 and half == nh - 1))
        alias_barrier(ffn_toks, L0_toks)

    def mixer0_half(NT, a0, kind, seq, g, first, last):
        T = NT * 128
        c0 = a0 * 128
        hsl = slice(c0, c0 + T)
        hTt = hT_t[a0:a0 + NT]
        L = 64 if kind == "p" else 32
        nb = 128 // L
        nseg = 1 if kind == "p" else 4
        Ls = 128 // nseg
        mask = maskP if kind == "p" else maskS
        bmcol = C_BMP if kind == "p" else C_BMS
        rst = cst[:, C_RSTP:C_RSTP + T] if kind == "p" else cst[:, C_RSTS:C_RSTS + 128]
        V = pvv
        if first:
            for h in range(4):
                memset(Sst[:, h, :], 0.0, [Sst_t[h]])
                memset(Sbf[:, h, :], 0.0, [Sbf_t[h]])
            memset(car[:], 0.0, [car_t])
        if kind == "s":
            xin = lt[0][:, 0:128].rearrange("p (a i b) -> p a i b", a=2, i=16)
            dma("sp", lt[0][:, 0:128], s5x0_d, writes=[lt_t[0]])
            ca4 = bc(V[:, V_CA:V_CA + 16], [128, 16, 4], 2)
            cb4 = bc(V[:, V_CB:V_CB + 16], [128, 16, 4], 2)
            rn4 = bc(V[:, V_RN2:V_RN2 + 16], [128, 16, 4], 2)
            ta = lt[1][:, 0:64].rearrange("p (i b) -> p i b", i=16)
            tb = lt[1][:, 64:128].rearrange("p (i b) -> p i b", i=16)
            tt(ta, xin[:, 0], ca4, ALU.mult, [lt_t[0], pvv_t], [lt_t[1]])
            tt(tb, xin[:, 1], cb4, ALU.mult, [lt_t[0], pvv_t], [lt_t[1]])
            tt(ta, ta, tb, ALU.add, [lt_t[1]], [lt_t[1]])
            tt(x0p[:, 0], ta, rn4, ALU.mult, [lt_t[1], pvv_t], [x0p_t])
            tt(ta, xin[:, 1], ca4, ALU.mult, [lt_t[0], pvv_t], [lt_t[1]])
            tt(tb, xin[:, 0], cb4, ALU.mult, [lt_t[0], pvv_t], [lt_t[1]])
            tt(ta, ta, tb, ALU.subtract, [lt_t[1]], [lt_t[1]])
            tt(x0p[:, 1], ta, rn4, ALU.mult, [lt_t[1], pvv_t], [x0p_t])
        uq, uq_t = load_unit(U_ABIN + 0)
        uf, uf_t = load_unit(U_ABIN + 1)
        uqv = uq[:].rearrange("p (q k f) -> p q k f", q=4, k=8)
        ufv = uf[:].rearrange("p (q k f) -> p q k f", q=4, k=8)
        nblk = T // L
        for h in range(4):
            bq, bf = pbank(), pbank()
            for k in range(8):
                mm(ps[bq][:, 0:T], uqv[:, h, k, :], hT[:, k, hsl], k == 0, k == 7, [uq_t] + hTt, [ps_t[bq]])
            for k in range(8):
                mm(ps[bf][:, 0:T], ufv[:, h, k, :], hT[:, k, hsl], k == 0, k == 7, [uf_t] + hTt, [ps_t[bf]])
            A_, B_, C_, D_ = [t[:, 0:T] for t in ht]
            act(A_, ps[bf][:, 0:T], AF.Sigmoid, [ps_t[bf]], [ht_t[0]])
            act(B_, ps[bf][:, 0:T], AF.Sigmoid, [ps_t[bf]], [ht_t[1]], scale=-1.0)
            ts(A_, A_, V[:, V_OML + h:V_OML + h + 1], V[:, V_LB + h:V_LB + h + 1], ALU.mult, ALU.add,
               [ht_t[0], pvv_t], [ht_t[0]])
            act(A_, A_, AF.Ln, [ht_t[0]], [ht_t[0]])
            scan(C_, rst, A_, 0.0, [ht_t[0], cst_t], [ht_t[2]])
            act(D_, C_, AF.Exp, [ht_t[2]], [ht_t[3]])
            tt(qs[:, h, 0:T], ps[bq][:, 0:T], D_, ALU.mult, [ps_t[bq], ht_t[3]], [qk_t[h]])
            act(A_, C_, AF.Exp, [ht_t[2]], [ht_t[0]], scale=-1.0)
            stt(ks[:, h, 0:T], B_, V[:, V_OML + h:V_OML + h + 1], A_, ALU.mult, ALU.mult,
                [ht_t[1], ht_t[0], pvv_t], [qk_t[h]])
            cp(eend[:, h, 0:nblk], D_.rearrange("p (b l) -> p b l", b=nblk)[:, :, L - 1], [ht_t[3]], [eend_t])
        ug, ug_t = load_unit(U_ABIN + 2)
        ugv = ug[:].rearrange("p (q k f) -> p q k f", q=4, k=8)
        for h in range(4):
            b = pbank()
            for k in range(8):
                mm(ps[b][:, 0:T], ugv[:, h, k, :], hT[:, k, hsl], k == 0, k == 7, [ug_t] + hTt, [ps_t[b]])
            act(gateT[:, h, 0:T], ps[b][:, 0:T], AF.Sigmoid, [ps_t[b]], [gate_t[h]])
        uu, uu_t = load_unit(U_ABIN + 3)
        uuv = uu[:].rearrange("p (q k f) -> p q k f", q=4, k=8)
        for c in range(4):
            b = pbank()
            for k in range(8):
                mm(ps[b][:, 0:T], uuv[:, c, k, :], hT[:, k, hsl], k == 0, k == 7, [uu_t] + hTt, [ps_t[b]])
            act(uT[:, c, 0:T], ps[b][:, 0:T], AF.Copy, [ps_t[b]], [uT_t[c]])
        ui, ui_t = load_unit(U_ABIN + 4)
        uiv = ui[:].rearrange("p (k n) -> p k n", k=8)
        for a in range(NT):
            b = pbank()
            for k in range(8):
                mm(ps[b][:, :], hT[:, k, c0 + a * 128:c0 + (a + 1) * 128], uiv[:, k, :], k == 0, k == 7,
                   [ui_t, hT_t[a0 + a]], [ps_t[b]])
            act(ivb[:, a, :], ps[b][:, :], AF.Copy, [ps_t[b]], [iv_t[a]])

        if stop == "m0p":
            return
        for a in range(NT):
            tsl = slice(a * 128, (a + 1) * 128)
            po = []
            for h in range(4):
                hs = slice(h * 128, (h + 1) * 128)
                b1 = pbank(True)
                mm(ps[b1][:, 0:128], ks[:, h, tsl], qs[:, h, tsl], True, True, [qk_t[h]], [ps_t[b1]])
                attm, attm_t = nsmb()
                tt(attm, ps[b1][:, 0:128], mask[:], ALU.mult, [ps_t[b1], mask_t], [attm_t])
                ck(1)
                kht, kht_t = nsmb()
                tt(kht.rearrange("p (b l) -> p b l", b=nb), ks[:, h, tsl].rearrange("p (b l) -> p b l", b=nb),
                   bc(eend[:, h, a * nb:(a + 1) * nb], [128, nb, L], 2), ALU.mult, [qk_t[h], eend_t], [kht_t])
                pv = ps[b1][:].bitcast(BF16)
                tr(pv[:, 512:640], kht, [kht_t], [ps_t[b1]])
                for i in range(nb):
                    ts(khm[:, i, :], pv[:, 512:640], cst[:, bmcol + i:bmcol + i + 1], None, ALU.mult, None,
                       [ps_t[b1], cst_t], [khm_t])
                ck(2)
                b2 = 4 + h
                po.append(b2)
                ck(3)
                for i in range(nb):
                    csl = slice(a * 128 + i * L, a * 128 + (i + 1) * L)
                    if kind == "s":
                        dma("sp", Sst[:, h, :], hs0_d[i, h], writes=[Sst_t[h]])
                        cp(Sbf[:, h, :], Sst[:, h, :], [Sst_t[h]], [Sbf_t[h]])
                    mm(ps[b2][:, i * L:(i + 1) * L], ivb[:, a, hs], attm[:, i * L:(i + 1) * L], True, False,
                       [iv_t[a], attm_t], [ps_t[b2]])
                    mm(ps[b2][:, i * L:(i + 1) * L], Sbf[:, h, :], qs[:, h, csl], False, True,
                       [Sbf_t[h], qk_t[h]], [ps_t[b2]])
                    b3 = pbank(True)
                    mm(ps[b3][:, 0:128], khm[:, i, :], ivb[:, a, hs], True, True, [khm_t, iv_t[a]], [ps_t[b3]])
                    blk = a * nb + i
                    stt(Sst[:, h, :], Sst[:, h, :], eend[:, h, blk:blk + 1], ps[b3][:, 0:128], ALU.mult, ALU.add,
                        [Sst_t[h], eend_t, ps_t[b3]], [Sst_t[h]])
                    ck(4)
                    if kind == "s":
                        dma("sp", o_hgs[i, h], Sst[:, h, :], reads=[Sst_t[h]])
                    else:
                        act(Sbf[:, h, :], Sst[:, h, :], AF.Copy, [Sst_t[h]], [Sbf_t[h]])
            ck(5)
            for h in range(4):
                b2 = po[h]
                osq, osq_t = nsmb()
                act(osq, ps[b2][:, 0:128], AF.Square, [ps_t[b2]], [osq_t])
                b4 = pbank(True)
                mm(ps[b4][:, 0:128], ones128[:], osq, True, True, [ones_t, osq_t], [ps_t[b4]])
                sd, sd_t = nsm()
                act(sd, ps[b4][:, 0:128], AF.Sqrt, [ps_t[b4]], [sd_t], bias=EPS)
                recip(sd, sd, [sd_t], [sd_t])
                y1, y1_t = nsm()
                tt(y1, ps[b2][:, 0:128], sd, ALU.mult, [ps_t[b2], sd_t], [y1_t])
                stt(abT[:, h, tsl], y1, prm[:, P_OUTG:P_OUTG + 1], gateT[:, h, tsl], ALU.mult, ALU.mult,
                    [y1_t, prm_t, gate_t[h]], [abT_t[h]])

        if stop == "m0h":
            return
        wglu = wsm[:, W_GLU:W_GLU + 2048].rearrange("p (c o f) -> p c o f", c=4, o=4)
        ubt, ubt_t = load_unit(U_BT)
        btre = ubt[:, 0:2048].rearrange("p (i s) -> p i s", i=16)
        btim = ubt[:, 2048:4096].rearrange("p (i s) -> p i s", i=16)

        def v4(ap512):
            return ap512.rearrange("p (i s j) -> p i s j", i=4, s=nseg)

        for a in range(NT):
            tsl = slice(a * 128, (a + 1) * 128)
            py = 4 + (a % 2)
            def issue_bu(c):
                u_b, u_bt = ubs[c % 2], ubs_t[c % 2]
                act(u_b, uT[:, c, tsl], AF.Copy, [uT_t[c]], [u_bt])
                br_, bi_ = pbank(True), pbank(True)
                for il in range(4):
                    mm(ps[br_][:, il * 128:(il + 1) * 128], btre[:, 4 * c + il, :], u_b, True, True, [ubt_t, u_bt], [ps_t[br_]])
                    mm(ps[bi_][:, il * 128:(il + 1) * 128], btim[:, 4 * c + il, :], u_b, True, True, [ubt_t, u_bt], [ps_t[bi_]])
                return br_, bi_

            nxt = issue_bu(0)
            for c in range(4):
                isl = slice(4 * c, 4 * c + 4)
                br_, bi_ = nxt
                if c + 1 < 4:
                    nxt = issue_bu(c + 1)
                cs4 = cosT[:, isl, 0:Ls].unsqueeze(2).to_broadcast([128, 4, nseg, Ls])
                sn4 = sinT[:, isl, 0:Ls].unsqueeze(2).to_broadcast([128, 4, nseg, Ls])
                t1, t1_t = nlt(); t2, t2_t = nlt(); wr_, wr_t = nlt(); wi_, wi_t = nlt()
                zr, zr_t = nlt(); zi, zi_t = nlt()
                rdt = [tab_t]
                tt(v4(t1), v4(ps[br_][:, :]), cs4, ALU.mult, [ps_t[br_]] + rdt, [t1_t])
                tt(v4(t2), v4(ps[bi_][:, :]), sn4, ALU.mult, [ps_t[bi_]] + rdt, [t2_t])
                tt(v4(zr), v4(ps[bi_][:, :]), cs4, ALU.mult, [ps_t[bi_]] + rdt, [zr_t])
                tt(v4(zi), v4(ps[br_][:, :]), sn4, ALU.mult, [ps_t[br_]] + rdt, [zi_t])
                tt(wr_, t1, t2, ALU.add, [t1_t, t2_t], [wr_t])
                tt(wi_, zr, zi, ALU.subtract, [zr_t, zi_t], [wi_t])
                for il in range(4):
                    i = 4 * c + il
                    for sgi in range(nseg):
                        sl = slice(il * 128 + sgi * Ls, il * 128 + (sgi + 1) * Ls)
                        if kind == "p":
                            ir, ii, it = car[:, 0, i, 0:1], car[:, 1, i, 0:1], car_t
                        else:
                            ir, ii, it = x0p[:, 0, i, sgi:sgi + 1], x0p[:, 1, i, sgi:sgi + 1], x0p_t
                        rbc = V[:, V_R + i:V_R + i + 1].to_broadcast([128, Ls])
                        scan(zr[:, sl], rbc, wr_[:, sl], ir, [wr_t, it, pvv_t], [zr_t])
                        scan(zi[:, sl], rbc, wi_[:, sl], ii, [wi_t, it, pvv_t], [zi_t])
                zre = v4(zr)[:, :, :, Ls - 1]
                zie = v4(zi)[:, :, :, Ls - 1]
                cL = bc(cosT[:, isl, Ls - 1], [128, 4, nseg], 2)
                sL = bc(sinT[:, isl, Ls - 1], [128, 4, nseg], 2)
                e1 = t1[:, 0:4 * nseg].rearrange("p (i s) -> p i s", i=4)
                e2 = t2[:, 0:4 * nseg].rearrange("p (i s) -> p i s", i=4)
                e3 = t1[:, 64:64 + 4 * nseg].rearrange("p (i s) -> p i s", i=4)
                e4 = t2[:, 64:64 + 4 * nseg].rearrange("p (i s) -> p i s", i=4)
                tt(e1, zre, cL, ALU.mult, [zr_t, tab_t], [t1_t])
                tt(e2, zie, sL, ALU.mult, [zi_t, tab_t], [t2_t])
                tt(e3, zre, sL, ALU.mult, [zr_t, tab_t], [t1_t])
                tt(e4, zie, cL, ALU.mult, [zi_t, tab_t], [t2_t])
                tt(car[:, 0, isl, 0:nseg], e1, e2, ALU.subtract, [t1_t, t2_t], [car_t])
                tt(car[:, 1, isl, 0:nseg], e3, e4, ALU.add, [t1_t, t2_t], [car_t])
                tt(v4(t1), v4(zr), cs4, ALU.mult, [zr_t, tab_t], [t1_t])
                tt(v4(t2), v4(zi), sn4, ALU.mult, [zi_t, tab_t], [t2_t])
                tt(v4(wr_), v4(zr), sn4, ALU.mult, [zr_t, tab_t], [wr_t])
                tt(v4(wi_), v4(zi), cs4, ALU.mult, [zi_t, tab_t], [wi_t])
                tt(xrb, t1, t2, ALU.subtract, [t1_t, t2_t], [xb_t])
                tt(xib, wr_, wi_, ALU.add, [wr_t, wi_t], [xb_t])
                for il in range(4):
                    i = 4 * c + il
                    mm(ps[py][:, c * 128:(c + 1) * 128], cre[:, i, :], xrb[:, il * 128:(il + 1) * 128], il == 0, False,
                       [cm_t, xb_t], [ps_t[py]])
                    mm(ps[py][:, c * 128:(c + 1) * 128], ncim[:, i, :], xib[:, il * 128:(il + 1) * 128], False, il == 3,
                       [cm_t, xb_t], [ps_t[py]])
            yd, yd_t = nlt()
            y2, y2_t = nlt()
            yd4 = yd.rearrange("p (c t) -> p c t", c=4)
            tt(yd4, uT[:, :, tsl], bc(prm[:, P_SD:P_SD + 4], [128, 4, 128], 2), ALU.mult, list(uT_t) + [prm_t], [yd_t])
            tt(yd, yd, ps[py][:, :], ALU.add, [yd_t, ps_t[py]], [yd_t])
            tt(y2, yd, yd, ALU.mult, [yd_t], [y2_t])
            ts(y2, y2, 0.044715, 1.0, ALU.mult, ALU.add, [y2_t], [y2_t])
            tt(y2, y2, yd, ALU.mult, [y2_t, yd_t], [y2_t])
            act(y2, y2, AF.Sigmoid, [y2_t], [y2_t], scale=1.5957691216057308)
            tt(zf.rearrange("p c t -> p (c t)"), yd, y2, ALU.mult, [yd_t, y2_t], list(z_t))
            act(zb.rearrange("p c t -> p (c t)"), zf.rearrange("p c t -> p (c t)"), AF.Copy, list(z_t), list(z_t))
            for fo in range(4):
                pg = pbank(True)
                for c in range(4):
                    mm(ps[pg][:, 0:128], wglu[:, c, fo, :], zb[:, c, :], c == 0, c == 3, [wsm_t, z_t[c]], [ps_t[pg]])
                s2, s2_t = nsm()
                act(s2, ps[pg][:, 0:128], AF.Sigmoid, [ps_t[pg], prm_t], [s2_t], bias=prm[:, P_BGLU + fo:P_BGLU + fo + 1])
                tt(abT[:, 4 + fo, tsl], zf[:, fo, :], s2, ALU.mult, [z_t[fo], s2_t], [abT_t[4 + fo]])

        if stop == "m0s":
            return
        out_proj(NT, a0, U_ABOUT, abT, abT_t)

        if last:
            if kind == "p":
                for h in range(4):
                    dma("sp", o_hgp[seq, h], Sst[:, h, :], reads=[Sst_t[h]])
            ca4 = bc(V[:, V_CA:V_CA + 16], [128, 16, nseg], 2)
            cb4 = bc(V[:, V_CB:V_CB + 16], [128, 16, nseg], 2)
            fin, fin_t = nlt()
            memset(fin[:, 0:128], 0.0, [fin_t])
            fr = fin[:, 0:64].rearrange("p (s i) -> p i s", s=4)[:, :, 0:nseg]
            fi = fin[:, 64:128].rearrange("p (s i) -> p i s", s=4)[:, :, 0:nseg]
            t1, t1_t = nlt()
            ea = t1[:, 0:16 * nseg].rearrange("p (i s) -> p i s", i=16)
            eb = t1[:, 64:64 + 16 * nseg].rearrange("p (i s) -> p i s", i=16)
            cr_, ci_ = car[:, 0, :, 0:nseg], car[:, 1, :, 0:nseg]
            tt(ea, cr_, ca4, ALU.mult, [car_t, pvv_t], [t1_t])
            tt(eb, ci_, cb4, ALU.mult, [car_t, pvv_t], [t1_t])
            tt(fr, ea, eb, ALU.subtract, [t1_t], [fin_t])
            tt(ea, ci_, ca4, ALU.mult, [car_t, pvv_t], [t1_t])
            tt(eb, cr_, cb4, ALU.mult, [car_t, pvv_t], [t1_t])
            tt(fi, ea, eb, ALU.add, [t1_t], [fin_t])
            pt = pbank()
            tr(ps[pt][:, 0:128], fin[:, 0:128], [fin_t], [ps_t[pt]], idn=ident32)
            ftr, ftr_t = nsm()
            cp(ftr, ps[pt][:, 0:128], [ps_t[pt]], [ftr_t])
            for part in range(2):
                for s_ in range(nseg):
                    row = part * 64 + s_ * 16
                    if kind == "p":
                        dma("sp", o_s5p[seq, part], ftr[row:row + 16, :], reads=[ftr_t])
                    else:
                        dma("sp", o_s5s[s_, part], ftr[row:row + 16, :], reads=[ftr_t])

    eend = sb("eend", [128, 4, 8]); eend_t = Tok()

    def run_group(kind, seq, g):
        NT = 4 if kind == "p" else 1
        for a in range(NT):
            src = xp[seq, (4 * g + a) * 128:(4 * g + a + 1) * 128, :] if kind == "p" else xs[:, :]
            dma("sp", x[:, a, :], src, writes=[x_t[a]])
        for l in range(2):
            rmsnorm_to_hT(NT, P_GAIN + (3 * l + 0) * 8)
            ffn(NT, U_FFN + (2 * l + 0) * 17)
            if stop == "ffn1_%d" % l:
                break
            rmsnorm_to_hT(NT, P_GAIN + (3 * l + 1) * 8)
            if l == 0:
                mixer0(NT, kind, seq, g)
            else:
                mixer1(NT, kind, seq, g)
            if stop == "mix_%d" % l or stop in ("pg", "sg", "m0p", "m0h", "m0s"):
                break
            if l == 1 and stop in ("pg1", "sg1"):
                break
            rmsnorm_to_hT(NT, P_GAIN + (3 * l + 2) * 8)
            ffn(NT, U_FFN + (2 * l + 1) * 17)
            if stop == "ffn2_%d" % l:
                break
        for a in range(NT):
            dst = yp[seq, (4 * g + a) * 128:(4 * g + a + 1) * 128, :] if kind == "p" else ys[:, :]
            dma("sp", dst, x[:, a, :], reads=[x_t[a]])

    KT = sb("KT", [128, 8, 2048], BF16); KT_t = toks(16)
    Vv = sb("Vv", [128, 16, 512], BF16); Vv_t = toks(16)
    wsTp = sb("wsTp", [128, 4, 128], BF16); wsTs = sb("wsTs", [128, 4, 128], BF16); wsT_t = Tok()
    gain1 = sb("gain1", [128, 192]); gain1_t = Tok()
    amask = sb("amask", [128, 128], BF16); smask = sb("smask", [128, 128], BF16); ones64 = sb("ones64", [128, 64], BF16)
    am_t = Tok()
    arena_off[0] = 0
    scratch_toks.clear()
    ugT, ug1_t = carve([128, 4, MT], BF16, 4)
    vnb, vnb_t = carve([128, 2, 512], BF16, 2)
    QT, QT_t = carve([128, 8, MT], BF16, 2)
    cdT, cdT_t = carve([128, 8, MT], BF16, 8)
    g1 = []
    g1_t = []
    for _i in range(4):
        _v, _t = carve([128, 512]); g1.append(_v); g1_t.append(_t)
    g1_rr = [0]
    qsb, qsb_t = carve([128, 8, 96])
    qsq, qsq_t = carve([128, 8, 96])
    qnb, qnb_t = carve([128, 8, 96], BF16)
    ksb, ksb_t = carve([128, 8, 96])
    cqn, cqn_t = carve([128, 256], BF16)
    cqT, cqT_t = carve([128, 2, 128], BF16)
    ckvf, ckvf_t = carve([128, 128])
    ckvb, ckvb_t = carve([128, 128], BF16)
    ckvT, ckvT_t = carve([128, 128], BF16)
    kper, kper_t = carve([128, 32])
    rt16 = []
    rt16_t = []
    for _i in range(2):
        _v, _t = carve([128, 8, 16]); rt16.append(_v); rt16_t.append(_t)
    PT = []
    PT_t = []
    for _i in range(3):
        _v, _t = carve([128, MT], BF16); PT.append(_v); PT_t.append(_t)
    PT_rr = [0]
    rden, rden_t = carve([128, MT])
    pck, pck_t = carve([128, 128])
    L1_toks = list(scratch_toks)
    L1_toks_ref.extend(L1_toks)

    def ng1():
        i = g1_rr[0] % 4
        g1_rr[0] += 1
        return g1[i], g1_t[i]

    def nPT():
        i = PT_rr[0] % 3
        PT_rr[0] += 1
        return PT[i], PT_t[i]

    def prep_layer1():
        ts(gain1[:, 0:96], prm[:, P_QGAIN:P_QGAIN + 96], float(96.0 ** -0.5), None, ALU.mult, None, [prm_t], [gain1_t])
        cp(gain1[:, 96:192], prm[:, P_KGAIN:P_KGAIN + 96], [prm_t], [gain1_t])
        cp(amask[:], cst[:, C_AMASK:C_AMASK + 128], [cst_t], [am_t])
        cp(smask[:], cst[:, C_SMASK:C_SMASK + 128], [cst_t], [am_t])
        cp(ones64[:], cst[:, C_ONES:C_ONES + 64], [cst_t], [am_t])
        gw, gw_t = g1[0], g1_t[0]
        gwv = gw.rearrange("p (g s) -> p g s", g=4)
        dma("sp", gw, gws_d, writes=[gw_t])
        gm, gm_t = g1[1], g1_t[1]
        gmb = gm.bitcast(BF16)[:, 0:512].rearrange("p (g s) -> p g s", g=4)
        tt(gmb, gwv, bc(cst[:, C_GTRILP:C_GTRILP + 128], [128, 4, 128], 1), ALU.mult, [gw_t, cst_t], [gm_t])
        b = pbank(True)
        pv = ps[b][:].bitcast(BF16)
        for gg in range(4):
            tr(pv[:, gg * 128:(gg + 1) * 128], gmb[:, gg, :], [gm_t], [ps_t[b]])
        cp(wsTp[:], pv[:, 0:512].rearrange("p (g t) -> p g t", g=4), [ps_t[b]], [wsT_t])
        gs, gs_t = g1[2], g1_t[2]
        memset(gs, 0.0, [gs_t])
        gsv = gs.rearrange("p (g s) -> p g s", g=4)
        gsrc = gws_d.rearrange("p (g s) -> p g s", g=4)
        for bb in range(4):
            dma("sp", gsv[32 * bb:32 * bb + 32, :, 32 * bb:32 * bb + 32], gsrc[0:32, :, 0:32], writes=[gs_t])
        gm2, gm2_t = g1[3], g1_t[3]
        gm2b = gm2.bitcast(BF16)[:, 0:512].rearrange("p (g s) -> p g s", g=4)
        tt(gm2b, gsv, bc(cst[:, C_GTRILS:C_GTRILS + 128], [128, 4, 128], 1), ALU.mult, [gs_t, cst_t], [gm2_t])
        b = pbank(True)
        pv = ps[b][:].bitcast(BF16)
        for gg in range(4):
            tr(pv[:, gg * 128:(gg + 1) * 128], gm2b[:, gg, :], [gm2_t], [ps_t[b]])
        cp(wsTs[:], pv[:, 0:512].rearrange("p (g t) -> p g t", g=4), [ps_t[b]], [wsT_t])

    alias_barrier(L1_toks, L0_toks + ffn_toks)
    prep_layer1()
    alias_barrier(L0_toks + ffn_toks, L1_toks)

    wuq = wsm[:, W_UQ:W_UQ + 1536].rearrange("p (k n) -> p k n", k=2)
    wukv = wsm[:, W_UKV:W_UKV + 1024]

    def gelu_from(src_ap, src_toks, out_ap, out_toks, n):
        yd, yd_t = ng1()
        y2, y2_t = ng1()
        act(yd[:, 0:n], src_ap, AF.Copy, src_toks, [yd_t])
        tt(y2[:, 0:n], yd[:, 0:n], yd[:, 0:n], ALU.mult, [yd_t], [y2_t])
        ts(y2[:, 0:n], y2[:, 0:n], 0.044715, 1.0, ALU.mult, ALU.add, [y2_t], [y2_t])
        tt(y2[:, 0:n], y2[:, 0:n], yd[:, 0:n], ALU.mult, [y2_t, yd_t], [y2_t])
        act(y2[:, 0:n], y2[:, 0:n], AF.Sigmoid, [y2_t], [y2_t], scale=1.5957691216057308)
        tt(out_ap, yd[:, 0:n], y2[:, 0:n], ALU.mult, [yd_t, y2_t], out_toks)

    def rms_stat(src_ap, src_toks, n, junk_ap, junk_toks):
        st, st_t = new_stat()
        act(junk_ap, src_ap, AF.Square, src_toks, list(junk_toks) + [st_t], accum=st[:, 0:1])
        act(st[:, 1:2], st[:, 0:1], AF.Sqrt, [st_t], [st_t], scale=1.0 / n, bias=EPS)
        recip(st[:, 2:3], st[:, 1:2], [st_t], [st_t])
        return st[:, 2:3], st_t

    def head_norm_T(srcsb, srcsb_t, gcol, dst, dst_tok, dcol):
        tt(qsq, srcsb, srcsb, ALU.mult, [srcsb_t], [qsq_t])
        ss, ss_t = rt16[0][:, :, 0], rt16_t[0]
        reduce_sum(ss, qsq, [qsq_t], [ss_t])
        act(ss, ss, AF.Sqrt, [ss_t], [ss_t], scale=1.0 / 96.0, bias=EPS)
        recip(ss, ss, [ss_t], [ss_t])
        tt(qsq, srcsb, bc(ss, [128, 8, 96], 2), ALU.mult, [srcsb_t, ss_t], [qsq_t])
        tt(qnb, qsq, bc(gain1[:, gcol:gcol + 96], [128, 8, 96], 1), ALU.mult, [qsq_t, gain1_t], [qnb_t])
        ck(134)
        b = pbank(True)
        pv = ps[b][:].bitcast(BF16)
        for h in range(8):
            tr(pv[0:96, h * 128:(h + 1) * 128], qnb[:, h, :], [qnb_t], [ps_t[b]])
        cp(dst[0:96, :, dcol:dcol + 128], pv[0:96, :].rearrange("p (h t) -> p h t", h=8), [ps_t[b]], [dst_tok])

    def rope(x1, x2, cos_b, sin_b, rd, wr, shape3):
        n = shape3
        if len(n) == 3:
            tv = [rt16[0][:, 0:4, :], rt16[0][:, 4:8, :], rt16[1][:, 0:4, :], rt16[1][:, 4:8, :]]
        else:
            tv = [rt16[0][:, 0, :], rt16[0][:, 4, :], rt16[1][:, 0, :], rt16[1][:, 4, :]]
        tk = [rt16_t[0], rt16_t[0], rt16_t[1], rt16_t[1]]
        tt(tv[0], x1, cos_b, ALU.mult, rd, [tk[0]])
        tt(tv[1], x2, sin_b, ALU.mult, rd, [tk[1]])
        tt(tv[2], x1, sin_b, ALU.mult, rd, [tk[2]])
        tt(tv[3], x2, cos_b, ALU.mult, rd, [tk[3]])
        tt(x1, tv[0], tv[1], ALU.subtract, [tk[0]], wr)
        tt(x2, tv[2], tv[3], ALU.add, [tk[2]], wr)

    def build_kv(slot):
        b = pbank(True)
        pv = ps[b][:].bitcast(BF16)
        tr(pv[:, 0:128], ckvb, [ckvb_t], [ps_t[b]])
        cp(ckvT, pv[:, 0:128], [ps_t[b]], [ckvT_t])
        for hfk in range(2):
            b = pbank(True)
            mm(ps[b][:, :], ckvT, wukv[:, hfk * 512:(hfk + 1) * 512], True, True, [ckvT_t, wsm_t], [ps_t[b]])
            kvv = ps[b][:, :].rearrange("p (h d) -> p h d", h=4)
            act(Vv[:, slot, hfk * 256:(hfk + 1) * 256].rearrange("p (h d) -> p h d", h=4), kvv[:, :, 64:128], AF.Copy,
                [ps_t[b]], [Vv_t[slot]])
            act(ksb[:, 4 * hfk:4 * hfk + 4, 0:64], kvv[:, :, 0:64], AF.Copy, [ps_t[b]], [ksb_t])
        cp(ksb[:, :, 64:96], bc(kper, [128, 8, 32], 1), [kper_t], [ksb_t])
        head_norm_T(ksb, ksb_t, 96, KT, KT_t[slot], slot * 128)

    def mixer1(NT, kind, seq, g):
        alias_barrier(L1_toks, ffn_toks + L0_toks)
        nh = (NT + 1) // 2
        for half in range(nh):
            a0 = 2 * half
            nt = min(2, NT - a0)
            mixer1_half(nt, a0, kind, seq, g)
        alias_barrier(ffn_toks + L0_toks, L1_toks)

    def mixer1_half(NT, a0, kind, seq, g):
        T = NT * 128
        c0 = a0 * 128
        hsl = slice(c0, c0 + T)
        hTt = hT_t[a0:a0 + NT]
        gbase = 4 * g + a0 if kind == "p" else 15
        wsT = wsTp if kind == "p" else wsTs
        bscol = P_BS if kind == "p" else P_BSS
        uu, uu_t = load_unit(U_CDIN + 0)
        uuv = uu[:].rearrange("p (q k f) -> p q k f", q=4, k=8)
        for cg in range(4):
            b = pbank(True)
            for k in range(8):
                mm(ps[b][:, 0:T], uuv[:, cg, k, :], hT[:, k, hsl], k == 0, k == 7, [uu_t] + hTt, [ps_t[b]])
            gelu_from(ps[b][:, 0:T], [ps_t[b]], ugT[:, cg, 0:T], [ug1_t[cg]], T)
        ck(11)
        uv, uv_t = load_unit(U_CDIN + 1)
        uvv = uv[:].rearrange("p (k n) -> p k n", k=8)
        for a in range(NT):
            b = pbank(True)
            for k in range(8):
                mm(ps[b][:, :], hT[:, k, c0 + a * 128:c0 + (a + 1) * 128], uvv[:, k, :], k == 0, k == 7,
                   [uv_t, hT_t[a0 + a]], [ps_t[b]])
            vg, vg_t = ng1()
            gelu_from(ps[b][:, :], [ps_t[b]], vg, [vg_t], 512)
            jk, jk_t = ng1()
            rs, rs_t = rms_stat(vg, [vg_t], 512, jk, [jk_t])
            vnf, vnf_t = ng1()
            stt(vnf, vg, rs, prm[:, P_VNORM:P_VNORM + 512], ALU.mult, ALU.mult, [vg_t, rs_t, prm_t], [vnf_t])
            act(vnb[:, a, :], vnf, AF.Copy, [vnf_t], [vnb_t[a]])
            if kind == "s":
                dma("sp", o_gvs, vnf, reads=[vnf_t])
            elif gbase + a == 15:
                dma("sp", o_gvp[seq], vnf, reads=[vnf_t])
        ck(12)
        ur, ur_t = load_unit(U_CDIN + 2)
        urv = ur[:].rearrange("p (k n) -> p k n", k=8)
        for a in range(NT):
            gt = gbase + a
            b = pbank(True)
            for k in range(8):
                mm(ps[b][:, 0:416], hT[:, k, c0 + a * 128:c0 + (a + 1) * 128], urv[:, k, 0:416], k == 0, k == 7,
                   [ur_t, hT_t[a0 + a]], [ps_t[b]])
            if kind == "p":
                cosr = cst[:, C_ROPEP + gt * 16:C_ROPEP + gt * 16 + 16]
                sinr = cst[:, C_ROPEP + 256 + gt * 16:C_ROPEP + 256 + gt * 16 + 16]
            else:
                cosr = cst[:, C_ROPES:C_ROPES + 16]
                sinr = cst[:, C_ROPES + 16:C_ROPES + 32]
            jk, jk_t = ng1()
            rs, rs_t = rms_stat(ps[b][:, 0:256], [ps_t[b]], 256, jk[:, 0:256], [jk_t])
            stt(cqn, ps[b][:, 0:256], rs, prm[:, P_QNORM:P_QNORM + 256], ALU.mult, ALU.mult,
                [ps_t[b], rs_t, prm_t], [cqn_t])
            rs2, rs2_t = rms_stat(ps[b][:, 256:384], [ps_t[b]], 128, jk[:, 256:384], [jk_t])
            stt(ckvf, ps[b][:, 256:384], rs2, prm[:, P_KVNORM:P_KVNORM + 128], ALU.mult, ALU.mult,
                [ps_t[b], rs2_t, prm_t], [ckvf_t])
            act(ckvb, ckvf, AF.Copy, [ckvf_t], [ckvb_t])
            act(kper, ps[b][:, 384:416], AF.Copy, [ps_t[b]], [kper_t])
            rope(kper[:, 0:16], kper[:, 16:32], cosr, sinr, [kper_t, cst_t], [kper_t], (128, 16))
            if kind == "p":
                dma("sp", o_ckvp[seq, gt * 128:(gt + 1) * 128, :], ckvf, reads=[ckvf_t])
                dma("sp", o_kpep[seq, gt * 128:(gt + 1) * 128, :], kper, reads=[kper_t])
            else:
                dma("sp", o_ckvs, ckvf, reads=[ckvf_t])
                dma("sp", o_kpes, kper, reads=[kper_t])
            ck(13)
            b2 = pbank(True)
            pv2 = ps[b2][:].bitcast(BF16)
            for kc in range(2):
                tr(pv2[:, kc * 128:(kc + 1) * 128], cqn[:, kc * 128:(kc + 1) * 128], [cqn_t], [ps_t[b2]])
            cp(cqT, pv2[:, 0:256].rearrange("p (k t) -> p k t", k=2), [ps_t[b2]], [cqT_t])
            ck(131)
            for cgp in range(2):
                b3 = pbank(True)
                for kc in range(2):
                    mm(ps[b3][:, 0:384], cqT[:, kc, :], wuq[:, kc, cgp * 384:(cgp + 1) * 384], kc == 0, kc == 1,
                       [cqT_t, wsm_t], [ps_t[b3]])
                qv = ps[b3][:, 0:384].rearrange("p (h d) -> p h d", h=4)
                hs4 = slice(4 * cgp, 4 * cgp + 4)
                act(qsb[:, hs4, :], qv, AF.Copy, [ps_t[b3]], [qsb_t])
                ck(132)
                rope(qsb[:, hs4, 64:80], qsb[:, hs4, 80:96],
                     bc(cosr, [128, 4, 16], 1), bc(sinr, [128, 4, 16], 1), [qsb_t, cst_t], [qsb_t], (128, 4, 16))
            ck(133)
            head_norm_T(qsb, qsb_t, 0, QT, QT_t[a], a * 128)
            ck(14)
            build_kv(gt)
        ck(15)
        for a in range(NT):
            b = pbank(True)
            for gg in range(4):
                mm(ps[b][:, gg * 128:(gg + 1) * 128], vnb[:, a, gg * 128:(gg + 1) * 128], wsT[:, gg, :], True, True,
                   [vnb_t[a], wsT_t], [ps_t[b]])
            tmp, tmp_t = ng1()
            tt(tmp, ps[b][:, :], prm[:, bscol:bscol + 512], ALU.add, [ps_t[b], prm_t], [tmp_t])
            tt(cdT[:, 0:4, a * 128:(a + 1) * 128], tmp.rearrange("p (g t) -> p g t", g=4), ugT[:, :, a * 128:(a + 1) * 128],
               ALU.mult, [tmp_t] + list(ug1_t), list(cdT_t[0:4]))
        ck(16)
        if kind == "p":
            gt_max = gbase + NT - 1
            for hp in range(4):
                bn, bd = 4 + 2 * (hp % 2), 5 + 2 * (hp % 2)
                for hh in range(2):
                    h = 2 * hp + hh
                    rows = slice(64 * hh, 64 * hh + 64)
                    tp = (0, 64) if hh == 1 else None
                    steps = [(qa, kt) for qa in range(NT) for kt in range(gbase + qa + 1)]

                    def att_s1(qa, kt, h=h):
                        cs_ = slice(qa * 128, (qa + 1) * 128)
                        last = (kt == gbase + qa)
                        bS = pbank(True)
                        mm(ps[bS][:, 0:128], KT[0:96, h, kt * 128:(kt + 1) * 128], QT[0:96, h, cs_], True, True,
                           [KT_t[kt], QT_t[qa]], [ps_t[bS]])
                        pt_, pt_t = nPT()
                        act(pt_[:, 0:128], ps[bS][:, 0:128], AF.Exp, [ps_t[bS]], [pt_t])
                        if last:
                            tt(pt_[:, 0:128], pt_[:, 0:128], amask[:], ALU.mult, [pt_t, am_t], [pt_t])
                        return pt_, pt_t

                    def att_s2(qa, kt, pt_, pt_t, h=h, rows=rows, tp=tp, bn=bn, bd=bd):
                        cs_ = slice(qa * 128, (qa + 1) * 128)
                        last = (kt == gbase + qa)
                        mm(ps[bn][rows, cs_], Vv[:, kt, h * 64:(h + 1) * 64], pt_[:, 0:128], kt == 0, last,
                           [Vv_t[kt], pt_t], [ps_t[bn]], tp=tp)
                        mm(ps[bd][rows, cs_], ones64[:], pt_[:, 0:128], kt == 0, last,
                           [am_t, pt_t], [ps_t[bd]], tp=tp)

                    prev = None
                    for (qa, kt) in steps:
                        cur = att_s1(qa, kt)
                        if prev is not None:
                            att_s2(*prev)
                        prev = (qa, kt) + cur
                    att_s2(*prev)
                recip(rden[:, 0:T], ps[bd][:, 0:T], [ps_t[bd]], [rden_t])
                tt(cdT[:, 4 + hp, 0:T], ps[bn][:, 0:T], rden[:, 0:T], ALU.mult, [ps_t[bn], rden_t], [cdT_t[4 + hp]])
        else:
            for sb_ in range(4):
                qc = slice(32 * sb_, 32 * sb_ + 32)
                for kt in range(8):
                    dma("sp", pck, pckv_d[sb_, kt * 128:(kt + 1) * 128, :], writes=[pck_t])
                    dma("sp", kper, pkpe_d[sb_, kt * 128:(kt + 1) * 128, :], writes=[kper_t])
                    act(ckvb, pck, AF.Copy, [pck_t], [ckvb_t])
                    build_kv(kt)
                for h in range(8):
                    hp, hh = h // 2, h % 2
                    bn = 4 + hp
                    rows = slice(64 * hh, 64 * hh + 64)
                    tp = (0, 64) if hh == 1 else None
                    oc = slice(hp * 128 + 32 * sb_, hp * 128 + 32 * sb_ + 32)

                    def sa_s1(slot, h=h, qc=qc):
                        bS = pbank(True)
                        mm(ps[bS][:, 0:32], KT[0:96, h, slot * 128:(slot + 1) * 128], QT[0:96, h, qc], True, True,
                           [KT_t[slot], QT_t[0]], [ps_t[bS]])
                        pt_, pt_t = nPT()
                        act(pt_[:, 0:32], ps[bS][:, 0:32], AF.Exp, [ps_t[bS]], [pt_t])
                        if slot == 15:
                            tt(pt_[:, 0:32], pt_[:, 0:32], smask[:, qc], ALU.mult, [pt_t, am_t], [pt_t])
                        return pt_, pt_t

                    def sa_s2(ki, slot, pt_, pt_t, h=h, rows=rows, tp=tp, oc=oc):
                        mm(ps[4][rows, oc], Vv[:, slot, h * 64:(h + 1) * 64], pt_[:, 0:32], ki == 0, ki == 8,
                           [Vv_t[slot], pt_t], [ps_t[4]], tp=tp)
                        mm(ps[5][rows, oc], ones64[:], pt_[:, 0:32], ki == 0, ki == 8,
                           [am_t, pt_t], [ps_t[5]], tp=tp)

                    prev = None
                    for ki, slot in enumerate([15] + list(range(8))):
                        cur = sa_s1(slot)
                        if prev is not None:
                            sa_s2(*prev)
                        prev = (ki, slot) + cur
                    sa_s2(*prev)
            for hp in range(4):
                hc = slice(hp * 128, (hp + 1) * 128)
                recip(rden[:, 0:128], ps[5][:, hc], [ps_t[5]], [rden_t])
                tt(cdT[:, 4 + hp, 0:128], ps[4][:, hc], rden[:, 0:128], ALU.mult, [ps_t[4], rden_t], [cdT_t[4 + hp]])
        ck(17)
        out_proj(NT, a0, U_CDOUT, cdT, cdT_t)


    groups = [("p", s, g) for s in range(2) for g in range(4)] + [("s", 0, 0)]
    if debug:
        groups = [("p", 0, 0), ("s", 0, 0)]
    if stop == "prep":
        groups = []
        dma("sp", ys[:, 0:128], cosT[:, 0, :], reads=[tab_t])
        dma("sp", ys[:, 128:256], sinT[:, 5, :], reads=[tab_t])
        dma("sp", ys[:, 256:512], pvv[:], reads=[pvv_t])
    if stop in ("pg", "m0p", "m0h", "m0s") or stop.startswith("ck"):
        groups = [("p", 0, 0)]
    if stop in ("sg", "sg1"):
        groups = [("s", 0, 0)]
    if stop == "pg1":
        groups = [("p", 0, 0)]
    try:
        for kind, seq, g in groups:
            run_group(kind, seq, g)
    except _Stop:
        pass

    S.finish()
    S.build()
    es.close()
    return nc


NRING = 3
U_FFN = 0
U_ABIN = 68
U_ABOUT = 73
U_CDIN = 75
U_CDOUT = 78
U_BT = 80
NUNITS = 81

C_IDENT = 0
C_MASKP = 128
C_MASKS = 256
C_BMP = 384
C_BMS = 386
C_JV = 390
C_RSTP = 518
C_RSTS = 1030
C_ONES = 1158
C_AMASK = 1286
C_SMASK = 1414
C_GTRILP = 1542
C_GTRILS = 1670
C_ROPEP = 1798
C_ROPES = 2310
NCONST = 2342

P_GAIN = 0
P_LRE = 48
P_LIM = 64
P_LOGDT = 80
P_SD = 96
P_BGLU = 100
P_LB = 104
P_OUTG = 116
P_VNORM = 117
P_QNORM = 629
P_KVNORM = 885
P_QGAIN = 1013
P_KGAIN = 1109
P_BS = 1205
P_BSS = 1717
NPARAM = 2229

W_GLU = 0
W_UQ = 2048
W_UKV = 3584
NSMALL = 4608


def host_consts():
    c = np.zeros((128, NCONST), np.float32)
    ar = np.arange(128)
    c[:, C_IDENT:C_IDENT + 128] = np.eye(128, dtype=np.float32)
    s_, t_ = ar[:, None], ar[None, :]
    c[:, C_MASKP:C_MASKP + 128] = ((s_ // 64 == t_ // 64) & (s_ <= t_)).astype(np.float32)
    c[:, C_MASKS:C_MASKS + 128] = ((s_ // 32 == t_ // 32) & (s_ <= t_)).astype(np.float32)
    c[:, C_BMP:C_BMP + 2] = (ar[:, None] // 64 == np.arange(2)[None, :]).astype(np.float32)
    c[:, C_BMS:C_BMS + 4] = (ar[:, None] // 32 == np.arange(4)[None, :]).astype(np.float32)
    c[:, C_JV:C_JV + 128] = (ar + 1).astype(np.float32)[None, :]
    c[:, C_RSTP:C_RSTP + 512] = (np.arange(512) % 64 != 0).astype(np.float32)[None, :]
    c[:, C_RSTS:C_RSTS + 128] = (ar % 32 != 0).astype(np.float32)[None, :]
    c[:, C_ONES:C_ONES + 128] = 1.0
    c[:, C_AMASK:C_AMASK + 128] = (s_ // 64 <= t_ // 64).astype(np.float32)
    c[:, C_SMASK:C_SMASK + 128] = (s_ // 32 == t_ // 32).astype(np.float32)
    c[:, C_GTRILP:C_GTRILP + 128] = (t_ <= s_).astype(np.float32)
    c[:, C_GTRILS:C_GTRILS + 128] = ((t_ <= s_) & (s_ // 32 == t_ // 32)).astype(np.float32)
    half = 16
    inv = (10000.0 ** (-np.arange(half, dtype=np.float32) / half)).astype(np.float32)
    pos = np.arange(SEQ, dtype=np.float32)
    ang = (pos[:, None] * inv[None, :]).astype(np.float32)
    cosp = np.cos(ang).astype(np.float32).reshape(16, 128, 16).transpose(1, 0, 2).reshape(128, 256)
    sinp = np.sin(ang).astype(np.float32).reshape(16, 128, 16).transpose(1, 0, 2).reshape(128, 256)
    c[:, C_ROPEP:C_ROPEP + 256] = cosp
    c[:, C_ROPEP + 256:C_ROPEP + 512] = sinp
    poss = (PAST + (ar % 32)).astype(np.float32)
    angs = (poss[:, None] * inv[None, :]).astype(np.float32)
    c[:, C_ROPES:C_ROPES + 16] = np.cos(angs)
    c[:, C_ROPES + 16:C_ROPES + 32] = np.sin(angs)
    return c


def host_params(inp):
    p = np.zeros((128, NPARAM), np.float32)
    for l in range(2):
        for wi, nm in enumerate(["ffn1_norm", "mix_norm", "ffn2_norm"]):
            col = P_GAIN + (3 * l + wi) * 8
            p[:, col:col + 8] = inp[nm][l].reshape(8, 128).T
    p[:, P_LRE:P_LRE + 16] = inp["s5_lambda_re"][0].reshape(16, 128).T
    p[:, P_LIM:P_LIM + 16] = inp["s5_lambda_im"][0].reshape(16, 128).T
    p[:, P_LOGDT:P_LOGDT + 16] = np.repeat(inp["s5_log_dt"][0], 64).reshape(16, 128).T
    p[:, P_SD:P_SD + 4] = inp["s5_d"][0].reshape(4, 128).T
    p[:, P_BGLU:P_BGLU + 4] = inp["s5_b_glu"][0].reshape(4, 128).T
    p[:, P_LB:P_LB + 12] = inp["hgrn_lb_logits"].reshape(3, 4, 128).transpose(2, 0, 1).reshape(128, 12)
    p[:, P_OUTG] = inp["hgrn_out_norm"][0]
    p[:, P_VNORM:P_VNORM + 512] = inp["gmlp_v_norm"][0][None, :]
    p[:, P_QNORM:P_QNORM + 256] = inp["mla_q_norm"][0][None, :]
    p[:, P_KVNORM:P_KVNORM + 128] = inp["mla_kv_norm"][0][None, :]
    p[:, P_QGAIN:P_QGAIN + 96] = inp["mla_q_gain"][0][None, :]
    p[:, P_KGAIN:P_KGAIN + 96] = inp["mla_k_gain"][0][None, :]
    bs = inp["gmlp_b_s"][0]
    p[:, P_BS:P_BS + 512] = bs.reshape(1, 512)
    p[:, P_BSS:P_BSS + 512] = np.tile(bs[:, :32], (1, 4)).reshape(1, 512)
    return p


def host_wsmall(inp):
    w = np.zeros((128, NSMALL), np.float32)
    w[:, W_GLU:W_GLU + 2048] = inp["s5_w_glu"][0].reshape(4, 128, 4, 128).transpose(1, 0, 2, 3).reshape(128, 2048)
    w[:, W_UQ:W_UQ + 1536] = inp["mla_w_uq"][0].reshape(2, 128, 768).transpose(1, 0, 2).reshape(128, 1536)
    w[:, W_UKV:W_UKV + 1024] = inp["mla_w_ukv"][0]
    return w


def host_cmats(inp):
    out = np.zeros((128, 2, 16, 128), np.float32)
    for pi, nm in enumerate(["s5_c_re", "s5_c_im"]):
        cm = inp[nm][0]
        for g in range(32):
            i = g // 2
            st = (g % 2) * 64
            cl = (g % 8) * 16
            out[st:st + 64, pi, i, cl:cl + 16] = cm[g].T
    return out.reshape(128, 4096)


def host_wstream(inp):
    ws = np.zeros((NUNITS, 128, 4096), np.float32)
    for l in range(2):
        for wi, pre in enumerate(["ffn1", "ffn2"]):
            wg = inp[pre + "_w_gate"][l]
            wu = inp[pre + "_w_up"][l]
            wdn = inp[pre + "_w_down"][l]
            g4 = wg.reshape(8, 128, NJ, 128).transpose(2, 1, 0, 3)
            u4 = wu.reshape(8, 128, NJ, 128).transpose(2, 1, 0, 3)
            base = U_FFN + (2 * l + wi) * 17
            for u in range(11):
                blk = np.stack([g4[2 * u], u4[2 * u], g4[2 * u + 1], u4[2 * u + 1]], axis=1)
                ws[base + u] = blk.reshape(128, 4096)
            d4 = wdn.reshape(NJ, 128, 2, 512)
            for hf in range(2):
                for ud in range(3):
                    nj = 8 if ud < 2 else 6
                    blk = np.zeros((128, 8, 512), np.float32)
                    blk[:, :nj, :] = d4[8 * ud:8 * ud + nj, :, hf, :].transpose(1, 0, 2)
                    ws[base + 11 + 3 * hf + ud] = blk.reshape(128, 4096)
    wab = inp["ab_w_in"][0]
    for k, col0 in enumerate([0, 512, 1536, 2048]):
        ws[U_ABIN + k] = fm_unit(wab, col0).reshape(128, 4096)
    ws[U_ABIN + 4] = tm_unit(wab, 1024, 512).reshape(128, 4096)
    for hf in range(2):
        ws[U_ABOUT + hf] = tm_unit(inp["ab_w_out"][0], hf * 512, 512).reshape(128, 4096)
        ws[U_CDOUT + hf] = tm_unit(inp["cd_w_out"][0], hf * 512, 512).reshape(128, 4096)
    for nm, col in (("s5_b_re", 0), ("s5_b_im", 2048)):
        bb = inp[nm][0]
        bt = np.zeros((128, 16, 128), np.float32)
        for g in range(32):
            i = g // 2
            cl = (g % 8) * 16
            st = (g % 2) * 64
            bt[cl:cl + 16, i, st:st + 64] = bb[g].T
        ws[U_BT, :, col:col + 2048] = bt.reshape(128, 2048)
    wcd = inp["cd_w_in"][0]
    ws[U_CDIN + 0] = fm_unit(wcd, 0).reshape(128, 4096)
    ws[U_CDIN + 1] = tm_unit(wcd, 512, 512).reshape(128, 4096)
    ws[U_CDIN + 2] = tm_unit(wcd, 1024, 416).reshape(128, 4096)
    return ws


_PROG = {}


def _get_prog(stop="full", debug=False):
    key = (stop, debug)
    if key not in _PROG:
        _PROG[key] = build_program(stop, debug)
    return _PROG[key]


def kernel(_stop="full", _debug=False, **inp):
    inp = {k: np.asarray(v, dtype=np.float32) for k, v in inp.items()}
    nc = _get_prog(_stop, _debug)
    consts = host_consts()
    params = host_params(inp)
    wstream = host_wstream(inp)
    wsmall = host_wsmall(inp)
    cmats = host_cmats(inp)
    gws = np.ascontiguousarray(inp["gmlp_w_s"][0].transpose(1, 0, 2).reshape(128, 512))
    in_maps = []
    for c in range(N_CORES):
        sb_ = slice(4 * c, 4 * c + 4)
        x0 = np.stack([inp["state_s5_re"][0, sb_].reshape(4, 16, 128),
                       inp["state_s5_im"][0, sb_].reshape(4, 16, 128)], axis=0)
        m = {
            "xp": np.ascontiguousarray(inp["x_prompt"][2 * c:2 * c + 2]),
            "xs": np.ascontiguousarray(inp["x_sample"][sb_].reshape(128, D)),
            "wstream": wstream, "consts": consts, "params": params, "wsmall": wsmall, "cmats": cmats, "gws": gws,
            "s5x0": np.ascontiguousarray(x0.transpose(3, 0, 2, 1).reshape(128, 128)),
            "hs0": np.ascontiguousarray(inp["state_hgrn"][0, sb_]),
            "pckv": np.ascontiguousarray(inp["cache_mla_ckv"][0, sb_]),
            "pkpe": np.ascontiguousarray(inp["cache_mla_kpe"][0, sb_]),
        }
        in_maps.append(m)
    res = run_bass_kernel_spmd(nc, in_maps, core_ids=list(range(N_CORES)))
    r = res.results

    def cat(name, shape):
        return np.concatenate([r[c][name].reshape(shape) for c in range(N_CORES)], axis=0)

    y_prompt = cat("yp", (2, SEQ, D))
    y_sample = cat("ys", (4, 32, D))
    hg_p = cat("o_hgp", (2, 4, 128, 128))[None]
    hg_s = cat("o_hgs", (4, 4, 128, 128))[None]
    s5p = cat("o_s5p", (2, 2, 32, 64))
    s5s = cat("o_s5s", (4, 2, 32, 64))
    gv_p = cat("o_gvp", (2, 128, 512))[None]
    gv_s = cat("o_gvs", (4, 32, 512))[None]
    ckv_p = cat("o_ckvp", (2, SEQ, 128))[None]
    kpe_p = cat("o_kpep", (2, SEQ, 32))[None]
    ckv_s = cat("o_ckvs", (4, 32, 128))[None]
    kpe_s = cat("o_kpes", (4, 32, 32))[None]
    return (y_prompt, y_sample, hg_p, hg_s,
            np.ascontiguousarray(s5p[:, 0])[None], np.ascontiguousarray(s5p[:, 1])[None],
            np.ascontiguousarray(s5s[:, 0])[None], np.ascontiguousarray(s5s[:, 1])[None],
            gv_p, gv_s, ckv_p, kpe_p, ckv_s, kpe_s)
```
